# Optimizing a Trainium2 kernel written in Bass

```python
import math
import jax
import jax.numpy as jnp
from jax import lax
import numpy as np

D_MODEL = 1024
BATCH = 8
SEQ = 2048
DEPTH = 2
DEC_BATCH = 32
DEC_SEQ = 64
PAST_LEN = 2048

CHUNK = 64
N_META = 16
MIX_WIDTH = 1024
POOL_WINDOWS = (2, 4, 8, 16)
POOL_GROUPS = len(POOL_WINDOWS)
POOL_GROUP_WIDTH = MIX_WIDTH // POOL_GROUPS
POOL_MAX = max(POOL_WINDOWS)
SB_HEADS = 8
SB_HEAD_DIM = MIX_WIDTH // SB_HEADS
SB_SCALE = SB_HEAD_DIM ** -0.5
QBLK = 128
HG_HEADS = 8
HG_DK = 128
HG_DV = MIX_WIDTH // HG_HEADS
N_BRANCH = 3
N_SPLIT = 13
EPS = 1e-6
LB_FLOOR = 1e-30

kernel_name = 'hybrid_pool_stickbreak_hgrn2_stream'


def rms_norm(x, g):
    xf = x.astype(jnp.float32)
    y = xf * lax.rsqrt(jnp.mean(xf * xf, axis=-1, keepdims=True) + EPS)
    return (y * g.astype(jnp.float32)).astype(x.dtype)


def pool_mixer(u, prefix, pos0, w_group, scale):
    b, l, _ = u.shape
    p = POOL_MAX - 1
    full = jnp.concatenate([prefix.astype(u.dtype), u], axis=1)
    csum = jnp.pad(jnp.cumsum(full.astype(jnp.float32), axis=1), ((0, 0), (1, 0), (0, 0)))
    pos = (pos0 + jnp.arange(l)).astype(jnp.float32)
    hi = csum[:, p + 1:p + 1 + l]
    means = []
    for gi, w in enumerate(POOL_WINDOWS):
        sl = slice(gi * POOL_GROUP_WIDTH, (gi + 1) * POOL_GROUP_WIDTH)
        lo = csum[:, p + 1 - w:p + 1 - w + l, sl]
        cnt = jnp.minimum(pos + 1.0, float(w))
        means.append((hi[..., sl] - lo) / cnt[None, :, None])
    pooled = jnp.stack(means, axis=2)
    diff = pooled - u.astype(jnp.float32).reshape(b, l, POOL_GROUPS, POOL_GROUP_WIDTH)
    y = jnp.einsum('blgc,gce->blge', diff.astype(u.dtype), w_group).reshape(b, l, MIX_WIDTH)
    return y * scale, full[:, -p:]


def stick_breaking(q, k, v, q_pos0):
    b, lq, h, d = q.shape
    lk = k.shape[1]
    qb = min(QBLK, lq)
    nblk = -(-lq // qb)
    qp = jnp.pad(q, ((0, 0), (0, nblk * qb - lq), (0, 0), (0, 0)))
    qblocks = qp.reshape(b, nblk, qb, h, d).transpose(1, 0, 2, 3, 4)
    kpos = jnp.arange(lk)

    def block(args):
        qblk, start = args
        z = jnp.einsum('bqhd,bkhd->bhqk', qblk, k).astype(jnp.float32) * SB_SCALE
        qpos = q_pos0 + start + jnp.arange(qb)
        mask = kpos[None, :] < qpos[:, None]
        log_keep = jnp.where(mask, jax.nn.log_sigmoid(-z), 0.0)
        later = lax.cumsum(log_keep, axis=3, reverse=True) - log_keep
        wts = jnp.where(mask, jnp.exp(jax.nn.log_sigmoid(z) + later), 0.0)
        return jnp.einsum('bhqk,bkhd->bqhd', wts.astype(v.dtype), v)

    out = lax.map(block, (qblocks, jnp.arange(nblk) * qb))
    return out.transpose(1, 0, 2, 3, 4).reshape(b, nblk * qb, h, d)[:, :lq]


def hgrn2(q, log_f, k, v, state0):
    b, l, h, _ = q.shape
    c = min(CHUNK, l)
    n = -(-l // c)
    pad = n * c - l
    def blocks(a):
        a = jnp.pad(a.astype(jnp.float32), ((0, 0), (0, pad), (0, 0), (0, 0)))
        return a.reshape(b, n, c, h, a.shape[-1]).transpose(1, 0, 3, 2, 4)
    causal = jnp.tril(jnp.ones((c, c), dtype=bool))[:, :, None]

    def step(s_prev, inp):
        qc, gc, kc, vc = inp
        cum = jnp.cumsum(gc, axis=2)
        o_inter = jnp.einsum('bhtk,bhkv->bhtv', qc * jnp.exp(cum), s_prev)
        gap = cum[:, :, :, None, :] - cum[:, :, None, :, :]
        decay = jnp.where(causal, jnp.exp(jnp.where(causal, gap, 0.0)), 0.0)
        att = jnp.einsum('bhtk,bhsk,bhtsk->bhts', qc, kc, decay)
        o = o_inter + jnp.einsum('bhts,bhsv->bhtv', att, vc)
        last = cum[:, :, -1:, :]
        s_new = jnp.exp(last[:, :, 0, :])[..., None] * s_prev + jnp.einsum('bhsk,bhsv->bhkv', kc * jnp.exp(last - cum), vc)
        return s_new, o

    s_fin, o = lax.scan(step, state0.astype(jnp.float32), (blocks(q), blocks(log_f), blocks(k), blocks(v)))
    o = o.transpose(1, 0, 3, 2, 4).reshape(b, n * c, h, -1)[:, :l]
    return o, s_fin


def mixer_layer(h, pos0, pool_prefix, k_past, v_past, hg_state, lb,
                norm_g, w_in, q_norm_g, k_norm_g, w_pool, pool_scale, hg_norm_g, w_branch, w_out):
    b, l, _ = h.shape
    xn = rms_norm(h, norm_g)
    (u_a, g_a, q_b, k_b, v_b, g_b, f_c, q_c, i_c, g_c,
     m_a, m_b, m_c) = jnp.split(xn @ w_in, N_SPLIT, axis=-1)
    heads = lambda t, d: t.reshape(b, l, -1, d)

    y_a, new_prefix = pool_mixer(u_a, pool_prefix, pos0, w_pool, pool_scale)
    y_a = y_a * jax.nn.silu(g_a)

    q = rms_norm(heads(q_b, SB_HEAD_DIM), q_norm_g)
    k = rms_norm(heads(k_b, SB_HEAD_DIM), k_norm_g)
    v = heads(v_b, SB_HEAD_DIM)
    if k_past is None:
        k_all, v_all, q_pos0 = k, v, 0
    else:
        k_all = jnp.concatenate([k_past.astype(k.dtype), k], axis=1)
        v_all = jnp.concatenate([v_past.astype(v.dtype), v], axis=1)
        q_pos0 = k_past.shape[1]
    y_b = stick_breaking(q, k_all, v_all, q_pos0).reshape(b, l, MIX_WIDTH) * jax.nn.silu(g_b)

    z = heads(f_c, HG_DK).astype(jnp.float32)
    lbh = lb.reshape(HG_HEADS, HG_DK).astype(jnp.float32)
    log_f = jnp.logaddexp(jnp.log(jnp.maximum(lbh, LB_FLOOR)), jnp.log1p(-lbh) + jax.nn.log_sigmoid(z))
    k_c = (1.0 - lbh) * jax.nn.sigmoid(-z)
    o_c, new_state = hgrn2(jax.nn.silu(heads(q_c, HG_DK)), log_f, k_c, heads(i_c, HG_DV), hg_state)
    o_c = rms_norm(o_c, hg_norm_g).astype(h.dtype).reshape(b, l, MIX_WIDTH)
    y_c = o_c * jax.nn.silu(g_c)

    ys = jnp.stack([y_a, y_b, y_c], axis=2)
    gates = jax.nn.sigmoid(jnp.stack([m_a, m_b, m_c], axis=2))
    merged = jnp.sum(gates * jnp.einsum('blnw,nwd->blnd', ys, w_branch), axis=2)
    h = h + merged @ w_out
    return h, k, v, new_prefix, new_state


def setup_inputs(seed: int = 0) -> dict:
    key = jax.random.key(seed)
    ks = jax.random.split(key, 17)
    nrm = lambda kk, shape, s: jax.random.normal(kk, shape, jnp.float32) * s
    return {
        'x_prompt': nrm(ks[0], (BATCH, SEQ, D_MODEL), 1.0),
        'x_sample': nrm(ks[1], (DEC_BATCH, DEC_SEQ, D_MODEL), 1.0),
        'cache_k': nrm(ks[2], (DEPTH, DEC_BATCH, PAST_LEN, SB_HEADS, SB_HEAD_DIM), 1.0),
        'cache_v': nrm(ks[3], (DEPTH, DEC_BATCH, PAST_LEN, SB_HEADS, SB_HEAD_DIM), 0.5),
        'state_pool': nrm(ks[4], (DEPTH, DEC_BATCH, POOL_MAX - 1, MIX_WIDTH), 1.0),
        'state_hgrn': nrm(ks[5], (DEPTH, DEC_BATCH, HG_HEADS, HG_DK, HG_DV), 0.5),
        'meta_tokens': nrm(ks[6], (N_META, D_MODEL), 1.0),
        'norm_g': 1.0 + nrm(ks[7], (DEPTH, D_MODEL), 0.05),
        'w_in': nrm(ks[8], (DEPTH, D_MODEL, N_SPLIT * MIX_WIDTH), D_MODEL ** -0.5),
        'q_norm_g': 1.0 + nrm(ks[9], (DEPTH, SB_HEAD_DIM), 0.05),
        'k_norm_g': 1.0 + nrm(ks[10], (DEPTH, SB_HEAD_DIM), 0.05),
        'w_pool': nrm(ks[11], (DEPTH, POOL_GROUPS, POOL_GROUP_WIDTH, POOL_GROUP_WIDTH), POOL_GROUP_WIDTH ** -0.5),
        'pool_scale': 1.0 + nrm(ks[12], (DEPTH, MIX_WIDTH), 0.05),
        'hgrn_lower_bounds': nrm(ks[13], (DEPTH, HG_HEADS * HG_DK), 0.1),
        'hgrn_norm_g': 1.0 + nrm(ks[14], (DEPTH, HG_DV), 0.05),
        'w_branch': nrm(ks[15], (DEPTH, N_BRANCH, MIX_WIDTH, D_MODEL), MIX_WIDTH ** -0.5),
        'w_out': nrm(ks[16], (DEPTH, D_MODEL, D_MODEL), D_MODEL ** -0.5),
    }


def reference(x_prompt, x_sample, cache_k, cache_v, state_pool, state_hgrn, meta_tokens,
              norm_g, w_in, q_norm_g, k_norm_g, w_pool, pool_scale, hgrn_lower_bounds,
              hgrn_norm_g, w_branch, w_out):
    lb_soft = jax.nn.softmax(hgrn_lower_bounds.astype(jnp.float32), axis=0)
    lower_bounds = jnp.cumsum(lb_soft, axis=0) - lb_soft[0:1]

    bp = x_prompt.shape[0]
    meta = jnp.broadcast_to(meta_tokens.astype(x_prompt.dtype)[None], (bp, N_META, D_MODEL))
    hp = jnp.concatenate([meta, x_prompt], axis=1)
    hs = x_sample
    past = cache_k.shape[2]
    kp, vp, pp, sp, ksm, vsm, psm, ssm = [], [], [], [], [], [], [], []
    for l in range(DEPTH):
        wl = (norm_g[l], w_in[l], q_norm_g[l], k_norm_g[l], w_pool[l], pool_scale[l],
              hgrn_norm_g[l], w_branch[l], w_out[l])
        hp, k_, v_, p_, s_ = mixer_layer(
            hp, 0, jnp.zeros((bp, POOL_MAX - 1, MIX_WIDTH), hp.dtype), None, None,
            jnp.zeros((bp, HG_HEADS, HG_DK, HG_DV), jnp.float32), lower_bounds[l], *wl)
        kp.append(k_); vp.append(v_); pp.append(p_); sp.append(s_.astype(hp.dtype))
        hs, k_, v_, p_, s_ = mixer_layer(
            hs, past, state_pool[l], cache_k[l], cache_v[l], state_hgrn[l], lower_bounds[l], *wl)
        ksm.append(k_); vsm.append(v_); psm.append(p_); ssm.append(s_.astype(state_hgrn.dtype))
    y_prompt = hp[:, N_META:]
    return (y_prompt, hs, jnp.stack(kp), jnp.stack(vp), jnp.stack(pp), jnp.stack(sp),
            jnp.stack(ksm), jnp.stack(vsm), jnp.stack(psm), jnp.stack(ssm))
```

```python
import contextlib
import os
import numpy as np
import concourse.bass as bass
import concourse.mybir as mybir
from concourse.bass_utils import run_bass_kernel_spmd

F32 = mybir.dt.float32
BF16 = mybir.dt.bfloat16
ALU = mybir.AluOpType
AF = mybir.ActivationFunctionType

NDMA_SEMS = 8
P = 128
EPS = 1e-6


class Trk:
    __slots__ = ("w", "r")

    def __init__(self):
        self.w = None
        self.r = []


class Op:
    __slots__ = ("eng", "fn", "deps", "is_dma", "needs_inc", "token", "dma_slot")

    def __init__(self, eng, fn, deps, is_dma):
        self.eng = eng
        self.fn = fn
        self.deps = deps
        self.is_dma = is_dma
        self.needs_inc = False
        self.token = None
        self.dma_slot = None


class Sched:
    ENGS = ("pe", "act", "dve", "pool", "sp")

    def __init__(self, nc):
        self.nc = nc
        self.q = {e: [] for e in self.ENGS}
        self.ndma = {e: 0 for e in self.ENGS}
        self.out_dmas = []

    def op(self, eng, fn, reads=(), writes=(), is_dma=False, is_out=False):
        deps = []
        seen = set()

        def add(o):
            if o is not None and id(o) not in seen:
                seen.add(id(o))
                deps.append(o)
        for t in reads:
            add(t.w)
        for t in writes:
            add(t.w)
            for r in t.r:
                add(r)
        o = Op(eng, fn, deps, is_dma)
        if is_dma:
            o.dma_slot = self.ndma[eng]
            self.ndma[eng] += 1
        for t in reads:
            if not is_dma:
                t.r = [r for r in t.r if r.is_dma or r.eng != eng]
            t.r.append(o)
        for t in writes:
            t.w = o
            t.r = []
        self.q[eng].append(o)
        if is_out:
            self.out_dmas.append(o)
        return o

    def emit(self, st):
        nc = self.nc
        esem = {e: st.enter_context(nc.semaphore("s_" + e)) for e in self.ENGS}
        dsem = {e: [st.enter_context(nc.semaphore("d_%s%d" % (e, i))) for i in range(NDMA_SEMS)]
                for e in self.ENGS if self.ndma[e] > 0}
        for e in self.ENGS:
            for o in self.q[e]:
                for d in o.deps:
                    if d.is_dma:
                        continue
                    if d.eng == "pe" and o.eng == "pe" and not o.is_dma:
                        continue
                    d.needs_inc = True
        for e in self.ENGS:
            c = 0
            for o in self.q[e]:
                if o.is_dma:
                    s = dsem[e][o.dma_slot % NDMA_SEMS]
                    o.token = (s, 16 * (o.dma_slot // NDMA_SEMS + 1))
                elif o.needs_inc:
                    c += 1
                    o.token = (esem[e], c)
        final = {}
        for e in self.ENGS:
            for o in self.q[e]:
                if o.is_dma:
                    final[id(o.token[0])] = o.token
        block = st.enter_context(nc.Block())

        def run_queue(e, h):
            waited = {}

            def wait(tok):
                s, v = tok
                k = id(s)
                if waited.get(k, 0) < v:
                    h.wait_ge(s, v)
                    waited[k] = v
            for o in self.q[e]:
                for d in o.deps:
                    if (not d.is_dma) and d.eng == "pe" and e == "pe" and not o.is_dma:
                        continue
                    wait(d.token)
                if o.is_dma and o.dma_slot >= NDMA_SEMS:
                    s, v = o.token
                    wait((s, v - 16))
                ins = o.fn(h)
                if o.is_dma:
                    ins.then_inc(o.token[0], 16)
                elif o.needs_inc:
                    ins.then_inc(o.token[0], 1)
            if e == "sp":
                for tok in final.values():
                    wait(tok)

        @block.tensor
        def _(h):
            run_queue("pe", h)

        @block.scalar
        def _(h):
            run_queue("act", h)

        @block.vector
        def _(h):
            run_queue("dve", h)

        @block.gpsimd
        def _(h):
            run_queue("pool", h)

        @block.sync
        def _(h):
            run_queue("sp", h)


class Cfg:
    def __init__(self, seq=2048, past=2048, depth=2):
        self.SEQ = seq
        self.PAST = past
        self.DEPTH = depth
        self.NBP = 1 + seq // 128
        self.NB = self.NBP + 2
        self.T = self.NB * 128
        self.TP = self.NBP * 128
        self.LP = 16 + seq
        self.tiles = [(s, min(512, self.T - s)) for s in range(0, self.T, 512)]
        self.NPC = past // 512


NVEC = 64


def build(cfg):
    nc = bass.Bass("TRN2", target_bir_lowering=False)
    D = cfg.DEPTH
    T, TP, NB, NBP = cfg.T, cfg.TP, cfg.NB, cfg.NBP
    tiles = cfg.tiles
    NT = len(tiles)

    def din(name, shape):
        return nc.dram_tensor(name, shape, F32, kind="ExternalInput").ap()

    def dout(name, shape):
        return nc.dram_tensor(name, shape, F32, kind="ExternalOutput").ap()

    xp = din("xp", [cfg.SEQ, 1024])
    xs = din("xs", [256, 1024])
    ck = din("ck", [D, 4, cfg.PAST, 8, 128])
    cv = din("cv", [D, 4, cfg.PAST, 8, 128])
    spool = din("spool", [D, 4, 15, 1024])
    shg = din("shg", [D, 4, 8, 128, 128])
    meta = din("meta", [16, 1024])
    vecs = din("vecs", [128, NVEC])
    consts = din("consts", [128, 64])
    w_in = din("w_in", [D, 1024, 13 * 1024])
    w_pool = din("w_pool", [D, 4, 256, 256])
    w_branch = din("w_branch", [D, 3, 1024, 1024])
    w_out = din("w_out", [D, 1024, 1024])

    yp = dout("yp", [cfg.SEQ, 1024])
    ys = dout("ys", [256, 1024])
    kp = dout("kp", [D, cfg.LP, 1024])
    vp = dout("vp", [D, cfg.LP, 1024])
    pp = dout("pp", [D, 15, 1024])
    hp = dout("hp", [D, 8, 128, 128])
    ks = dout("ks", [D, 256, 1024])
    vs = dout("vs", [D, 256, 1024])
    pls = dout("pls", [D, 4, 15, 1024])
    hs = dout("hs", [D, 4, 8, 128, 128])
    hd = nc.dram_tensor("hscr", [128, 8, T], F32).ap()

    S = Sched(nc)
    st = contextlib.ExitStack()

    def sb(name, shape, dt):
        return st.enter_context(nc.sbuf_tensor(name, shape, dt))

    def psb(name):
        return st.enter_context(nc.psum_tensor(name, [128, 512], F32))

    xnT = st.enter_context(nc.sbuf_tensor("xnT", [128, 8, T], BF16, side="right"))
    yT = st.enter_context(nc.sbuf_tensor("yT", [128, 8, T], BF16, side="right"))
    tmpT = st.enter_context(nc.sbuf_tensor("tmpT", [128, 8, T], BF16, side="right"))
    xn_t = [Trk() for _ in range(NB)]
    y_t = [[Trk() for _ in range(NB)] for _ in range(8)]
    tmp_t = [[Trk() for _ in range(NT)] for _ in range(8)]
    hd_t = [[Trk() for _ in range(NT)] for _ in range(8)]
    NWS = 8
    wbf = [sb("wbf%d" % i, [128, 8, 128], BF16) for i in range(NWS)]
    wbf_t = [Trk() for _ in range(NWS)]
    wst = [sb("wst%d" % i, [128, 8, 128], F32) for i in range(2)]
    wst_t = [Trk() for _ in range(2)]
    stg = [sb("stg%d" % i, [128, 1024], F32) for i in range(2)]
    stg_t = [Trk() for _ in range(2)]
    NWK = 8
    wk = [sb("wk%d" % i, [128, 512], F32) for i in range(NWK)]
    wk_t = [Trk() for _ in range(NWK)]
    ident = sb("ident", [128, 128], F32)
    onesf = sb("onesf", [128, 128], F32)
    negtri = sb("negtri", [128, 128], BF16)
    negones = sb("negones", [128, 128], BF16)
    mask01 = sb("mask01", [128, 128], BF16)
    maskle2 = sb("maskle2", [128, 2, 64], F32)
    onesb = sb("onesb", [128, 512], BF16)
    vec = sb("vec", [128, NVEC], F32)
    cst = sb("cst", [128, 64], F32)
    lbv = sb("lbv", [128, 4, D * 8], F32)
    gqs = sb("gqs", [128, D], F32)
    c_t = Trk()
    ps = [psb("ps%d" % i) for i in range(8)]
    ps_t = [Trk() for _ in range(8)]
    B_PA, B_PB, B_ST, B_TR, B_Z0, B_Z1, B_AV, B_X = range(8)

    def blk(c0, n):
        return range(c0 // 128, (c0 + n + 127) // 128)

    def xnr(c0, n):
        return [xn_t[b] for b in blk(c0, n)]

    def yw(c, c0, n):
        return [y_t[c][b] for b in blk(c0, n)]

    def yr(c0, n):
        return [y_t[c][b] for c in range(8) for b in blk(c0, n)]

    def dma(out, in_, reads=(), writes=(), is_out=False, q="sp"):
        return S.op(q, lambda h: h.dma_start(out=out, in_=in_), reads, writes, is_dma=True, is_out=is_out)

    def mm(out, lhsT, rhs, start, stop, reads, writes, sgc=False):
        return S.op("pe", lambda h: h.matmul(out, lhsT=lhsT, rhs=rhs, start=start, stop=stop,
                                             skip_group_check=sgc), reads, writes)

    def tr(out, in_, reads, writes):
        k = in_.shape[0]
        return S.op("pe", lambda h: h.transpose(out, in_, ident[0:k, 0:k]), list(reads) + [c_t], writes)

    def act(out, in_, func, reads, writes, bias=None, scale=None):
        kw = {}
        if bias is not None:
            kw["bias"] = bias
        if scale is not None:
            kw["scale"] = scale
        return S.op("act", lambda h: h.activation(out, in_, func, **kw), reads, writes)

    def tt(eng, out, a, b, op, reads, writes):
        return S.op(eng, lambda h: h.tensor_tensor(out, a, b, op), reads, writes)

    def ts(eng, out, a, s1, s2, op0, op1, reads, writes):
        if op1 is None:
            return S.op(eng, lambda h: h.tensor_scalar(out, a, s1, None, op0), reads, writes)
        return S.op(eng, lambda h: h.tensor_scalar(out, a, s1, s2, op0, op1), reads, writes)

    def stt(eng, out, a, s, b, op0, op1, reads, writes):
        return S.op(eng, lambda h: h.scalar_tensor_tensor(out, a, s, b, op0, op1), reads, writes)

    def cp(eng, out, in_, reads, writes):
        if eng == "act":
            return S.op("act", lambda h: h.copy(out, in_), reads, writes)
        return S.op(eng, lambda h: h.tensor_copy(out, in_), reads, writes)

    def memset(eng, ap, v, writes):
        return S.op(eng, lambda h: h.memset(ap, v), (), writes)

    class WStream:
        def __init__(self):
            self.plan = []
            self.issued = 0
            self.taken = 0
            self.LOOK = 3

        def add(self, src, nk=8):
            self.plan.append((src, nk))

        def _issue(self):
            i = self.issued
            src, nk = self.plan[i]
            s_i = i % 2
            b_i = i % NWS
            dma(wst[s_i][:, 0:nk, :], src, writes=[wst_t[s_i]])
            cp("pool", wbf[b_i][:, 0:nk, :], wst[s_i][:, 0:nk, :], [wst_t[s_i]], [wbf_t[b_i]])
            self.issued += 1

        def get(self):
            i = self.taken
            while self.issued < len(self.plan) and self.issued <= i + self.LOOK:
                self._issue()
            self.taken += 1
            return wbf[i % NWS], wbf_t[i % NWS]

    W = WStream()

    def wsrc_in(l, grp, c):
        col = grp * 1024 + c * 128
        return w_in[l].rearrange("(k p) n -> p k n", p=128)[:, :, col:col + 128]

    def wsrc_br(l, n, c):
        return w_branch[l, n].rearrange("(k p) n -> p k n", p=128)[:, :, c * 128:(c + 1) * 128]

    def wsrc_out(l, c):
        return w_out[l].rearrange("(k p) n -> p k n", p=128)[:, :, c * 128:(c + 1) * 128]

    def wsrc_pool(l, g, e):
        return w_pool[l, g].rearrange("(k p) n -> p k n", p=128)[:, :, e * 128:(e + 1) * 128]

    G_UA, G_GA, G_QB, G_KB, G_VB, G_GB, G_FC, G_QC, G_IC, G_GC, G_MA, G_MB, G_MC = range(13)

    def plan_layer(l):
        for g in range(4):
            W.add(wsrc_in(l, G_UA, 2 * g))
            W.add(wsrc_in(l, G_UA, 2 * g + 1))
            for e in range(2):
                W.add(wsrc_pool(l, g, e), 2)
                W.add(wsrc_in(l, G_GA, 2 * g + e))
        plan_merge(l, 0, G_MA)
        for hh in range(8):
            W.add(wsrc_in(l, G_KB, hh))
            W.add(wsrc_in(l, G_VB, hh))
            W.add(wsrc_in(l, G_QB, hh))
            W.add(wsrc_in(l, G_GB, hh))
        plan_merge(l, 1, G_MB)
        for hh in range(8):
            W.add(wsrc_in(l, G_FC, hh))
            W.add(wsrc_in(l, G_QC, hh))
            W.add(wsrc_in(l, G_IC, hh))
            W.add(wsrc_in(l, G_GC, hh))
        plan_merge(l, 2, G_MC)

    def plan_merge(l, n, gm):
        for c in range(8):
            W.add(wsrc_br(l, n, c))
            W.add(wsrc_in(l, gm, c))
        for c in range(8):
            W.add(wsrc_out(l, c))

    def proj(wslot, wt, c0, n, bank, nk=8, src=None, src_reads=None):
        srcT = xnT if src is None else src
        rd = xnr(c0, n) if src_reads is None else src_reads
        for k in range(nk):
            mm(ps[bank][:, 0:n], wslot[:, k, :], srcT[:, k, c0:c0 + n], k == 0, k == nk - 1,
               [wt] + list(rd), [ps_t[bank]])

    def rstd_from(bank, n, scale, out_wk):
        act(wk[out_wk][:, 0:n], ps[bank][:, 0:n], AF.Ln, [ps_t[bank]], [wk_t[out_wk]], bias=EPS, scale=scale)
        act(wk[out_wk][:, 0:n], wk[out_wk][:, 0:n], AF.Exp, [wk_t[out_wk]], [wk_t[out_wk]], scale=-0.5)

    dma(vec[:], vecs[:, :], writes=[c_t])
    dma(cst[:], consts[:, :], writes=[c_t])
    memset("pool", onesf[:], 1.0, [c_t])
    memset("pool", onesb[:], 1.0, [c_t])
    memset("pool", negones[:], -1.0, [c_t])
    S.op("pool", lambda h: h.affine_select(ident[:], onesf[:], pattern=[[-1, 128]], compare_op=ALU.is_equal,
                                           fill=0.0, base=0, channel_multiplier=1), [c_t], [c_t])
    S.op("pool", lambda h: h.affine_select(negtri[:], negones[:], pattern=[[-1, 128]], compare_op=ALU.is_ge,
                                           fill=0.0, base=0, channel_multiplier=1), [c_t], [c_t])
    S.op("pool", lambda h: h.affine_select(mask01[:], onesb[:, 0:128], pattern=[[1, 128]], compare_op=ALU.is_gt,
                                           fill=0.0, base=0, channel_multiplier=-1), [c_t], [c_t])
    S.op("pool", lambda h: h.affine_select(maskle2[:], onesf[:].rearrange("p (a b) -> p a b", a=2),
                                           pattern=[[64, 2], [1, 64]], compare_op=ALU.is_ge, fill=0.0, base=0,
                                           channel_multiplier=-1), [c_t], [c_t])
    V_NG, V_PS, V_LB, V_GQ = 0, 8 * D, 16 * D, 24 * D
    V_GK, V_GH = V_GQ + D, V_GQ + 2 * D
    lbraw = vec[:, V_LB:V_LB + 8 * D].rearrange("p (l c) -> p l c", l=D)
    mx = wk[0][:, 0:8]
    ex = wk[0][:, 8:8 + 8 * D].rearrange("p (l c) -> p l c", l=D)
    sm = wk[0][:, 200:208]
    cp("dve", mx, lbraw[:, 0, :], [c_t], [wk_t[0]])
    for l in range(1, D):
        tt("dve", mx, mx, lbraw[:, l, :], ALU.max, [c_t, wk_t[0]], [wk_t[0]])
    for l in range(D):
        tt("dve", ex[:, l, :], lbraw[:, l, :], mx, ALU.subtract, [c_t, wk_t[0]], [wk_t[0]])
    act(wk[0][:, 8:8 + 8 * D], wk[0][:, 8:8 + 8 * D], AF.Exp, [wk_t[0]], [wk_t[0]])
    cp("dve", sm, ex[:, 0, :], [wk_t[0]], [wk_t[0]])
    for l in range(1, D):
        tt("dve", sm, sm, ex[:, l, :], ALU.add, [wk_t[0]], [wk_t[0]])
    S.op("dve", lambda h: h.reciprocal(sm, sm), [wk_t[0]], [wk_t[0]])
    for l in range(D):
        tt("dve", ex[:, l, :], ex[:, l, :], sm, ALU.mult, [wk_t[0]], [wk_t[0]])
    lb4 = lbv[:].rearrange("p f (l c) -> p f l c", l=D)
    memset("dve", lbv[:, 0, 0:8], 0.0, [c_t])
    for l in range(1, D):
        tt("dve", lb4[:, 0, l, :], lb4[:, 0, l - 1, :], ex[:, l, :], ALU.add, [wk_t[0], c_t], [c_t])
    ts("dve", lbv[:, 1, :], lbv[:, 0, :], -1.0, 1.0, ALU.mult, ALU.add, [c_t], [c_t])
    ts("dve", lbv[:, 2, :], lbv[:, 0, :], 1e-30, None, ALU.max, None, [c_t], [c_t])
    ts("dve", lbv[:, 3, :], lbv[:, 1, :], -1.0, None, ALU.mult, None, [c_t], [c_t])
    ts("dve", gqs[:], vec[:, V_GQ:V_GQ + D], float(128 ** -0.5), None, ALU.mult, None, [c_t], [c_t])

    def store_h_block(b, si):
        for half in range(2):
            bank = B_PA + half
            for j in range(4):
                c = half * 4 + j
                tr(ps[bank][:, j * 128:(j + 1) * 128], stg[si][:, c * 128:(c + 1) * 128], [stg_t[si]], [ps_t[bank]])
            o = wk[half]
            cp("act" if half else "dve", o[:], ps[bank][:], [ps_t[bank]], [wk_t[half]])
            ti = (b * 128) // 512
            dma(hd[:, half * 4:half * 4 + 4, b * 128:(b + 1) * 128],
                o[:].rearrange("p (j n) -> p j n", j=4), [wk_t[half]], [hd_t[c][ti] for c in range(half * 4, half * 4 + 4)])

    for b in range(NB):
        si = b % 2
        if b == 0:
            memset("pool", stg[si][:], 0.0, [stg_t[si]])
            dma(stg[si][112:128, :], meta[:, :], writes=[stg_t[si]])
        elif b < NBP:
            dma(stg[si][:], xp[(b - 1) * 128:b * 128, :], writes=[stg_t[si]])
        else:
            dma(stg[si][:], xs[(b - NBP) * 128:(b - NBP + 1) * 128, :], writes=[stg_t[si]])
        store_h_block(b, si)

    def norm_phase(l):
        for ti, (c0, n) in enumerate(tiles):
            for c in range(8):
                hb = wk[2 + (c % 2)]
                hbt = wk_t[2 + (c % 2)]
                dma(hb[:, 0:n], hd[:, c, c0:c0 + n], [hd_t[c][ti]], [hbt])
                act(hb[:, 0:n], hb[:, 0:n], AF.Square, [hbt], [hbt])
                mm(ps[B_ST][:, 0:n], onesf[:], hb[:, 0:n], c == 0, c == 7, [hbt, c_t], [ps_t[B_ST]])
            rstd_from(B_ST, n, 1.0 / 1024.0, 4)
            for c in range(8):
                hb = wk[5 + (c % 2)]
                hbt = wk_t[5 + (c % 2)]
                dma(hb[:, 0:n], hd[:, c, c0:c0 + n], [hd_t[c][ti]], [hbt])
                stt("dve", xnT[:, c, c0:c0 + n], hb[:, 0:n], vec[:, V_NG + l * 8 + c:V_NG + l * 8 + c + 1],
                    wk[4][:, 0:n], ALU.mult, ALU.mult, [hbt, wk_t[4], c_t], xnr(c0, n))

    def merge_phase(l, nbr):
        pairs = [(B_PA, B_PB), (B_Z0, B_Z1), (B_AV, B_X), (B_ST, B_TR)]
        wks = [0, 1, 6, 7]
        it = 0
        for c in range(8):
            wb, wbt = W.get()
            wm, wmt = W.get()
            for ti, (c0, n) in enumerate(tiles):
                pa, pb = pairs[it % 4]
                wi_ = wks[it % 4]
                it += 1
                proj(wb, wbt, c0, n, pa, src=yT, src_reads=yr(c0, n))
                proj(wm, wmt, c0, n, pb)
                act(wk[wi_][:, 0:n], ps[pb][:, 0:n], AF.Sigmoid, [ps_t[pb]], [wk_t[wi_]])
                tt("dve", tmpT[:, c, c0:c0 + n], ps[pa][:, 0:n], wk[wi_][:, 0:n], ALU.mult,
                   [ps_t[pa], wk_t[wi_]], [tmp_t[c][ti]])
        banks = [B_PA, B_PB, B_Z0, B_Z1]
        it = 0
        for c in range(8):
            wo, wot = W.get()
            for ti, (c0, n) in enumerate(tiles):
                bank = banks[it % 4]
                hb = wk[2 + (it % 4)]
                hbt = wk_t[2 + (it % 4)]
                it += 1
                dma(hb[:, 0:n], hd[:, c, c0:c0 + n], [hd_t[c][ti]], [hbt])
                for k in range(8):
                    mm(ps[bank][:, 0:n], wo[:, k, :], tmpT[:, k, c0:c0 + n], k == 0, k == 7,
                       [wot, tmp_t[k][ti]], [ps_t[bank]])
                tt("dve", hb[:, 0:n], hb[:, 0:n], ps[bank][:, 0:n], ALU.add, [hbt, ps_t[bank]], [hbt])
                dma(hd[:, c, c0:c0 + n], hb[:, 0:n], [hbt], [hd_t[c][ti]])

    bar = sb("bar", [128, 8], F32)
    last_bar = [None]

    def PT():
        t = Trk()
        t.w = last_bar[0]
        return t

    def branch_a(l, ph):
        U = ph.enter_context(nc.sbuf_tensor("pa_U_%d" % l, [128, T], F32))
        X = ph.enter_context(nc.sbuf_tensor("pa_X_%d" % l, [128, T], F32))
        Y = ph.enter_context(nc.sbuf_tensor("pa_Y_%d" % l, [128, T], F32))
        Dd = tmpT
        us = ph.enter_context(nc.sbuf_tensor("pa_us_%d" % l, [128, 4, 80], F32))
        ux = ph.enter_context(nc.sbuf_tensor("pa_ux_%d" % l, [128, 4, 80], F32))
        uy = ph.enter_context(nc.sbuf_tensor("pa_uy_%d" % l, [128, 4, 80], F32))
        ppre = ph.enter_context(nc.sbuf_tensor("pa_pre_%d" % l, [128, 4, 8, 16], F32))
        pst = ph.enter_context(nc.sbuf_tensor("pa_pst_%d" % l, [16, 5, 128], F32))
        U_t, X_t, Y_t, us_t, ux_t, uy_t, pre_t, pst_t = [PT() for _ in range(8)]
        D_t = [list(tmp_t[0]), list(tmp_t[1])]
        for s in range(4):
            si = s % 2
            dma(stg[si][0:15, :], spool[l, s], writes=[stg_t[si]])
            for c in range(8):
                tr(ps[B_TR][:, c * 16:c * 16 + 15], stg[si][0:15, c * 128:(c + 1) * 128], [stg_t[si]], [ps_t[B_TR]])
            cp("dve", ppre[:, s, :, 0:15], ps[B_TR][:, 0:128].rearrange("p (c r) -> p c r", r=16)[:, :, 0:15],
               [ps_t[B_TR]], [pre_t])
        for g in range(4):
            w = 2 << g
            nlev = g + 1
            for j in range(2):
                c = 2 * g + j
                wu, wut = W.get()
                for ti, (c0, n) in enumerate(tiles):
                    bank = B_PA + (ti % 2)
                    proj(wu, wut, c0, n, bank)
                    cp("act", U[:, c0:c0 + n], ps[bank][:, 0:n], [ps_t[bank]], [U_t])
                segs = [TP - 15] + [TP + 64 * s + 49 for s in range(4)]
                for i, s0 in enumerate(segs[0:4]):
                    tr(ps[B_TR][0:15, i * 128:(i + 1) * 128], U[:, s0:s0 + 15], [U_t], [ps_t[B_TR]])
                cp("dve", pst[0:15, 0:4, :], ps[B_TR][0:15, 0:512].rearrange("p (i n) -> p i n", i=4),
                   [ps_t[B_TR]], [pst_t])
                tr(ps[B_TR][0:15, 0:128], U[:, segs[4]:segs[4] + 15], [U_t], [ps_t[B_TR]])
                cp("dve", pst[0:15, 4, :], ps[B_TR][0:15, 0:128], [ps_t[B_TR]], [pst_t])
                dma(pp[l, :, c * 128:(c + 1) * 128], pst[0:15, 0, :], [pst_t], [], is_out=True)
                dma(pls[l, :, :, c * 128:(c + 1) * 128].rearrange("s r n -> r s n"), pst[0:15, 1:5, :], [pst_t], [],
                    is_out=True)
                cp("pool", us[:, :, 0:15], ppre[:, :, c, 0:15], [pre_t], [us_t])
                cp("pool", us[:, :, 15:79], U[:, TP:T].rearrange("p (s n) -> p s n", s=4), [U_t], [us_t])
                src, srct = U, U_t
                ssrc, ssrct = us, us_t
                bufs = [(X, X_t), (Y, Y_t)]
                sbufs = [(ux, ux_t), (uy, uy_t)]
                sh = 1
                for lev in range(nlev):
                    lo = 2 * sh - 1
                    dst, dstt = bufs[lev % 2]
                    tt("dve", dst[:, lo:T], src[:, lo:T], src[:, lo - sh:T - sh], ALU.add, [srct], [dstt])
                    sdst, sdstt = sbufs[lev % 2]
                    tt("pool", sdst[:, :, lo:79], ssrc[:, :, lo:79], ssrc[:, :, lo - sh:79 - sh], ALU.add,
                       [ssrct], [sdstt])
                    src, srct = dst, dstt
                    ssrc, ssrct = sdst, sdstt
                    sh *= 2
                memset("pool", Dd[:, j, 0:16], 0.0, D_t[j])
                stt("dve", Dd[:, j, 16:TP], src[:, 16:TP], 1.0 / w, U[:, 16:TP], ALU.mult, ALU.subtract,
                    [srct, U_t], D_t[j])
                tt("dve", X[:, 0:16] if src is not X else Y[:, 0:16], src[:, 112:128], cst[:, g * 16:g * 16 + 16],
                   ALU.mult, [srct, c_t], [X_t if src is not X else Y_t])
                fixb = X if src is not X else Y
                fixt = X_t if src is not X else Y_t
                tt("dve", Dd[:, j, 112:128], fixb[:, 0:16], U[:, 112:128], ALU.subtract, [fixt, U_t], D_t[j])
                stt("dve", Dd[:, j, TP:T].rearrange("p (s n) -> p s n", s=4), ssrc[:, :, 15:79], 1.0 / w,
                    us[:, :, 15:79], ALU.mult, ALU.subtract, [ssrct, us_t], D_t[j])
            for e in range(2):
                co = 2 * g + e
                wp_, wpt = W.get()
                wg, wgt = W.get()
                for ti, (c0, n) in enumerate(tiles):
                    pa, pb = [(B_PA, B_PB), (B_Z0, B_Z1), (B_AV, B_X)][ti % 3]
                    wi_ = [0, 1, 6][ti % 3]
                    for k in range(2):
                        mm(ps[pa][:, 0:n], wp_[:, k, :], Dd[:, k, c0:c0 + n], k == 0, k == 1,
                           [wpt] + D_t[k], [ps_t[pa]])
                    proj(wg, wgt, c0, n, pb)
                    act(wk[wi_][:, 0:n], ps[pb][:, 0:n], AF.Silu, [ps_t[pb]], [wk_t[wi_]])
                    stt("dve", yT[:, co, c0:c0 + n], ps[pa][:, 0:n], vec[:, V_PS + l * 8 + co:V_PS + l * 8 + co + 1],
                        wk[wi_][:, 0:n], ALU.mult, ALU.mult, [ps_t[pa], wk_t[wi_], c_t], yw(co, c0, n))
        return [U_t, X_t, Y_t, us_t, ux_t, uy_t, pre_t, pst_t]

    def branch_b(l, ph):
        def a(name, shape, dt):
            return ph.enter_context(nc.sbuf_tensor(name + "_%d" % l, shape, dt))
        knT = a("pb_knT", [128, T], BF16)
        Vtok = a("pb_Vtok", [128, NBP, 128], BF16)
        Vs = a("pb_Vs", [64, 4, 128], BF16)
        qn = a("pb_qn", [128, 512], BF16)
        sp = [a("pb_sp%d" % i, [128, 512], BF16) for i in range(2)]
        wT = [a("pb_wT%d" % i, [128, 512], BF16) for i in range(2)]
        ew = [a("pb_ew%d" % i, [128, 512], F32) for i in range(2)]
        sacc = a("pb_sacc", [128, 512], BF16)
        cs = a("pb_cs", [128, 4, 64], BF16)
        kst = [a("pb_kst%d" % i, [128, 4, 128], F32) for i in range(2)]
        vst = [a("pb_vst%d" % i, [128, 4, 128], F32) for i in range(2)]
        kcT = [a("pb_kcT%d" % i, [128, 512], BF16) for i in range(2)]
        vc = [a("pb_vc%d" % i, [128, 4, 128], BF16) for i in range(2)]
        knT_t, V_t, Vs_t, qn_t, sacc_t, cs_t = [PT() for _ in range(6)]
        sp_t = [PT(), PT()]
        wT_t = [PT(), PT()]
        ew_t = [PT(), PT()]
        kst_t = [PT(), PT()]
        vst_t = [PT(), PT()]
        kcT_t = [PT(), PT()]
        vc_t = [PT(), PT()]
        alltr = [knT_t, V_t, Vs_t, qn_t, sacc_t, cs_t] + sp_t + wT_t + ew_t + kst_t + vst_t + kcT_t + vc_t
        stgi = [0]

        def out_rows(dst_p, dst_s, l, hh, srcbuf, srct, c0, n):
            pass

        for hh in range(int(os.environ.get('KH', '8'))):
            KV = int(os.environ.get('KV', '9'))
            for which in range(2):
                wslot, wt_ = W.get()
                dst_p, dst_s = (kp, ks) if which == 0 else (vp, vs)
                for ti, (c0, n) in (enumerate(tiles) if KV >= 1 else []):
                    bank = B_PA + (ti % 2)
                    proj(wslot, wt_, c0, n, bank)
                    xi_ = 0 if ti % 2 == 0 else 6
                    qi_ = 1 if ti % 2 == 0 else 7
                    xf, xft = wk[xi_], wk_t[xi_]
                    cp("act", xf[:, 0:n], ps[bank][:, 0:n], [ps_t[bank]], [xft])
                    if which == 0:
                        act(wk[qi_][:, 0:n], xf[:, 0:n], AF.Square, [xft], [wk_t[qi_]])
                        mm(ps[B_ST][:, 0:n], onesf[:], wk[qi_][:, 0:n], True, True, [wk_t[qi_], c_t], [ps_t[B_ST]])
                        rstd_from(B_ST, n, 1.0 / 128.0, 4)
                        stt("dve", xf[:, 0:n], xf[:, 0:n], vec[:, V_GK + l:V_GK + l + 1], wk[4][:, 0:n],
                            ALU.mult, ALU.mult, [xft, wk_t[4], c_t], [xft])
                        cp("pool", knT[:, c0:c0 + n], xf[:, 0:n], [xft], [knT_t])
                    for b in (blk(c0, n) if (KV >= 2 and int(os.environ.get('KW', which)) == which) else []):
                        o0 = b * 128 - c0
                        si = stgi[0] % 2
                        stgi[0] += 1
                        if (b < NBP and KV == 7) or (b >= NBP and KV == 6):
                            continue
                        tb_ = B_TR if si == 0 else B_X
                        if b < NBP:
                            tr(ps[tb_][:, 0:128], xf[:, o0:o0 + 128], [xft], [ps_t[tb_]])
                            cp("dve", stg[si][:, 0:128], ps[tb_][:, 0:128], [ps_t[tb_]], [stg_t[si]])
                            if which == 1:
                                cp("pool", Vtok[:, b, :], stg[si][:, 0:128], [stg_t[si]], [V_t])
                            if b == 0 and KV == 3:
                                pass
                            elif b == 0:
                                (lambda *a, **k: None if (KV == 5 or os.environ.get("NODMA")) else dma(*a, **k))(dst_p[l, 0:16, hh * 128:(hh + 1) * 128], stg[si][112:128, 0:128], [stg_t[si]], [],
                                    is_out=True)
                            else:
                                r0 = 16 + (b - 1) * 128
                                (lambda *a, **k: None if (KV == 5 or os.environ.get("NODMA")) else dma(*a, **k))(dst_p[l, r0:r0 + 128, hh * 128:(hh + 1) * 128], stg[si][:, 0:128], [stg_t[si]], [],
                                    is_out=True)
                        else:
                            tr(ps[tb_][:, 0:128], xf[:, o0:o0 + 128], [xft], [ps_t[tb_]])
                            cp("dve", stg[si][:, 0:128], ps[tb_][:, 0:128], [ps_t[tb_]], [stg_t[si]])
                            r0 = (b - NBP) * 128
                            dma(dst_s[l, r0:r0 + 128, hh * 128:(hh + 1) * 128], stg[si][:, 0:128], [stg_t[si]], [],
                                is_out=True)
                if which == 1:
                    for s_ in range(4):
                        k0 = TP + s_ * 64
                        for k in range(8):
                            mm(ps[B_TR][0:64, 0:128], xnT[:, k, k0:k0 + 64], wslot[:, k, :], k == 0, k == 7,
                               [wt_] + xnr(k0, 64), [ps_t[B_TR]])
                        cp("act", Vs[:, s_, :], ps[B_TR][0:64, 0:128], [ps_t[B_TR]], [Vs_t])
            wq, wqt = W.get()
            wg, wgt = W.get()
            KB = int(os.environ.get('KB', '9'))

            def qproj(c0, n):
                proj(wq, wqt, c0, n, B_PA)
                cp("act", wk[0][:, 0:n], ps[B_PA][:, 0:n], [ps_t[B_PA]], [wk_t[0]])
                act(wk[1][:, 0:n], wk[0][:, 0:n], AF.Square, [wk_t[0]], [wk_t[1]])
                mm(ps[B_ST][:, 0:n], onesf[:], wk[1][:, 0:n], True, True, [wk_t[1], c_t], [ps_t[B_ST]])
                rstd_from(B_ST, n, 1.0 / 128.0, 4)
                stt("dve", qn[:, 0:n], wk[0][:, 0:n], gqs[:, l:l + 1], wk[4][:, 0:n], ALU.mult, ALU.mult,
                    [wk_t[0], wk_t[4], c_t], [qn_t])

            def gate_out(c0, n, avcols):
                proj(wg, wgt, c0, n, B_PB)
                act(wk[5][:, 0:n], ps[B_PB][:, 0:n], AF.Silu, [ps_t[B_PB]], [wk_t[5]])
                tt("dve", yT[:, hh, c0:c0 + n], ps[B_AV][:, avcols:avcols + n], wk[5][:, 0:n], ALU.mult,
                   [ps_t[B_AV], wk_t[5]], yw(hh, c0, n))

            for b0 in (range(0, NBP, 4) if KB >= 2 else []):
                b1 = min(b0 + 4, NBP)
                nq = (b1 - b0) * 128
                c0 = b0 * 128
                qproj(c0, nq)
                memset("pool", sacc[:, 0:nq], 0.0, [sacc_t])
                kbs = list(range(b1 - 1, -1, -1))
                state = {"first_av": True}

                def stage1(i, kb):
                    pi = i % 2
                    zb = B_Z0 + pi
                    qo = (max(kb, b0) - b0) * 128
                    mm(ps[zb][:, qo:nq], knT[:, kb * 128:(kb + 1) * 128], qn[:, qo:nq], True, True,
                       [knT_t, qn_t], [ps_t[zb]])
                    act(ew[pi][:, qo:nq], ps[zb][:, qo:nq], AF.Exp, [ps_t[zb]], [ew_t[pi]])
                    act(sp[pi][:, qo:nq], ew[pi][:, qo:nq], AF.Ln, [ew_t[pi]], [sp_t[pi]], bias=1.0)
                    if kb >= b0:
                        tt("pool", sp[pi][:, qo:qo + 128], sp[pi][:, qo:qo + 128], mask01[:], ALU.mult,
                           [sp_t[pi], c_t], [sp_t[pi]])

                def stage2(i, kb):
                    pi = i % 2
                    zb = B_Z0 + pi
                    qo = (max(kb, b0) - b0) * 128
                    first = (i == 0)
                    mm(ps[zb][:, qo:nq], knT[:, kb * 128:(kb + 1) * 128], qn[:, qo:nq], True, False,
                       [knT_t, qn_t, ew_t[pi]], [ps_t[zb]])
                    mm(ps[zb][:, qo:nq], negtri[:], sp[pi][:, qo:nq], False, first, [sp_t[pi], c_t, ps_t[zb]],
                       [ps_t[zb]])
                    if not first:
                        mm(ps[zb][:, qo:nq], negones[:], sacc[:, qo:nq], False, True, [sacc_t, c_t, ps_t[zb]],
                           [ps_t[zb]])
                    tt("pool", sacc[:, qo:nq], sacc[:, qo:nq], sp[pi][:, qo:nq], ALU.add, [sacc_t, sp_t[pi]], [sacc_t])
                    act(wT[pi][:, qo:nq], ps[zb][:, qo:nq], AF.Exp, [ps_t[zb]], [wT_t[pi]])
                    if kb >= b0:
                        tt("pool", wT[pi][:, qo:qo + 128], wT[pi][:, qo:qo + 128], mask01[:], ALU.mult,
                           [wT_t[pi], c_t], [wT_t[pi]])
                    mm(ps[B_AV][:, qo:nq], Vtok[:, kb, :], wT[pi][:, qo:nq], first, i == len(kbs) - 1,
                       [V_t, wT_t[pi], ps_t[B_AV]], [ps_t[B_AV]], sgc=True)

                stage1(0, kbs[0])
                for i, kb in enumerate(kbs):
                    if i + 1 < len(kbs):
                        stage1(i + 1, kbs[i + 1])
                    stage2(i, kb)
                gate_out(c0, nq, 0)

            qproj(TP, 256)
            qns = qn
            for s in (range(4) if KB >= 3 else []):
                qc = qns[:, s * 64:(s + 1) * 64]
                kc0 = TP + s * 64
                pieces = list(range(cfg.NPC - 1, -1, -1))

                def load_piece(i):
                    pc = pieces[i]
                    bi = i % 2
                    dma(kst[bi][:], ck[l, s, pc * 512:(pc + 1) * 512, hh, :].rearrange("(j p) d -> p j d", p=128),
                        [], [kst_t[bi]], q="act")
                    dma(vst[bi][:], cv[l, s, pc * 512:(pc + 1) * 512, hh, :].rearrange("(j p) d -> p j d", p=128),
                        [], [vst_t[bi]], q="act")
                load_piece(0)
                zb = B_Z0
                mm(ps[zb][0:64, 0:64], knT[:, kc0:kc0 + 64], qc, True, True, [knT_t, qn_t], [ps_t[zb]])
                act(ew[0][0:64, 0:64], ps[zb][0:64, 0:64], AF.Exp, [ps_t[zb]], [ew_t[0]])
                act(sp[0][0:64, 0:64], ew[0][0:64, 0:64], AF.Ln, [ew_t[0]], [sp_t[0]], bias=1.0)
                tt("pool", sp[0][0:64, 0:64], sp[0][0:64, 0:64], mask01[0:64, 0:64], ALU.mult, [sp_t[0], c_t], [sp_t[0]])
                mm(ps[zb][0:64, 0:64], knT[:, kc0:kc0 + 64], qc, True, False, [knT_t, qn_t, ew_t[0]], [ps_t[zb]])
                mm(ps[zb][0:64, 0:64], negtri[0:64, 0:64], sp[0][0:64, 0:64], False, True, [sp_t[0], c_t, ps_t[zb]],
                   [ps_t[zb]])
                memset("pool", sacc[:, 0:64], 0.0, [sacc_t])
                cp("pool", sacc[0:64, 0:64], sp[0][0:64, 0:64], [sp_t[0]], [sacc_t])
                act(wT[0][0:64, 0:64], ps[zb][0:64, 0:64], AF.Exp, [ps_t[zb]], [wT_t[0]])
                tt("pool", wT[0][0:64, 0:64], wT[0][0:64, 0:64], mask01[0:64, 0:64], ALU.mult, [wT_t[0], c_t], [wT_t[0]])
                mm(ps[B_AV][:, 0:64], Vs[:, s, :], wT[0][0:64, 0:64], True, False, [Vs_t, wT_t[0]], [ps_t[B_AV]])
                for i, pc in enumerate(pieces):
                    bi = i % 2
                    pi = (i + 1) % 2
                    zb = B_Z0 + pi
                    if i + 1 < len(pieces):
                        load_piece(i + 1)
                    tb_ = B_TR if bi == 0 else B_X
                    for j in range(4):
                        tr(ps[tb_][:, j * 128:(j + 1) * 128], kst[bi][:, j, :], [kst_t[bi]], [ps_t[tb_]])
                    cp("dve", kcT[bi][:], ps[tb_][:], [ps_t[tb_]], [kcT_t[bi]])
                    cp("pool", vc[bi][:], vst[bi][:], [vst_t[bi]], [vc_t[bi]])
                    for j in range(4):
                        mm(ps[zb][:, j * 64:(j + 1) * 64], kcT[bi][:, j * 128:(j + 1) * 128], qc, j == 0, j == 3,
                           [kcT_t[bi], qn_t], [ps_t[zb]])
                    act(ew[pi][:, 0:256], ps[zb][:, 0:256], AF.Exp, [ps_t[zb]], [ew_t[pi]])
                    act(sp[pi][:, 0:256], ew[pi][:, 0:256], AF.Ln, [ew_t[pi]], [sp_t[pi]], bias=1.0)
                    sp3 = sp[pi][:, 0:256].rearrange("p (j n) -> p j n", j=4)
                    cp("pool", cs[:, 3, :], sacc[:, 0:64], [sacc_t], [cs_t])
                    for j in (2, 1, 0):
                        tt("pool", cs[:, j, :], cs[:, j + 1, :], sp3[:, j + 1, :], ALU.add, [cs_t, sp_t[pi]], [cs_t])
                    tt("pool", sacc[:, 0:64], cs[:, 0, :], sp3[:, 0, :], ALU.add, [cs_t, sp_t[pi]], [sacc_t])
                    for j in range(4):
                        mm(ps[zb][:, j * 64:(j + 1) * 64], kcT[bi][:, j * 128:(j + 1) * 128], qc, j == 0, False,
                           [kcT_t[bi], qn_t, ew_t[pi]], [ps_t[zb]])
                    mm(ps[zb][:, 0:256], negtri[:], sp[pi][:, 0:256], False, False, [sp_t[pi], c_t, ps_t[zb]],
                       [ps_t[zb]])
                    mm(ps[zb][:, 0:256], negones[:], cs[:].rearrange("p j n -> p (j n)"), False, True,
                       [cs_t, c_t, ps_t[zb]], [ps_t[zb]])
                    act(wT[pi][:, 0:256], ps[zb][:, 0:256], AF.Exp, [ps_t[zb]], [wT_t[pi]])
                    for j in range(4):
                        mm(ps[B_AV][:, 0:64], vc[bi][:, j, :], wT[pi][:, j * 64:(j + 1) * 64], False,
                           (i == len(pieces) - 1) and j == 3, [vc_t[bi], wT_t[pi], ps_t[B_AV]], [ps_t[B_AV]])
                gate_out(kc0, 64, 0)
        return alltr

    def branch_c(l, ph):
        def a(name, shape, dt):
            return ph.enter_context(nc.sbuf_tensor(name + "_%d" % l, shape, dt))
        G = a("pc_G", [128, 640], F32)
        kcf = a("pc_kcf", [128, 512], F32)
        ktb = a("pc_ktb", [128, 512], BF16)
        qtb = a("pc_qtb", [128, 512], BF16)
        ktok = a("pc_ktok", [128, 4, 128], BF16)
        vtok = a("pc_vtok", [128, 4, 128], BF16)
        fac = a("pc_fac", [128, 3, 8], F32)
        Sa = [a("pc_S%d" % i, [128, 128], F32) for i in range(2)]
        Sst = a("pc_Sst", [128, 128], F32)
        Stmp = a("pc_Stmp", [128, 128], F32)
        Sbf = [a("pc_Sbf%d" % i, [128, 128], BF16) for i in range(2)]
        attb = [a("pc_attb%d" % i, [128, 64], BF16) for i in range(2)]
        G_t, kcf_t, ktb_t, qtb_t, ktok_t, vtok_t, fac_t, Sst_t, Stmp_t = [PT() for _ in range(9)]
        Sa_t = [PT(), PT()]
        Sbf_t = [PT(), PT()]
        attb_t = [PT(), PT()]
        alltr = [G_t, kcf_t, ktb_t, qtb_t, ktok_t, vtok_t, fac_t, Sst_t, Stmp_t] + Sa_t + Sbf_t + attb_t
        nchunk_total = T // 64
        for hh in range(8):
            wf, wft = W.get()
            wq, wqt = W.get()
            wi, wit = W.get()
            wg, wgt = W.get()
            col = l * 8 + hh
            oml = lbv[:, 1, col:col + 1]
            lbf = lbv[:, 2, col:col + 1]
            noml = lbv[:, 3, col:col + 1]
            memset("dve", G[:, 0:1], 0.0, [G_t])
            memset("dve", Sa[0][:], 0.0, [Sa_t[0]])
            cur = 0
            sidx = 0
            for ti, (c0, n) in enumerate(tiles):
                nch = n // 64
                proj(wf, wft, c0, n, B_PA)
                act(wk[0][:, 0:n], ps[B_PA][:, 0:n], AF.Sigmoid, [ps_t[B_PA]], [wk_t[0]])
                ts("dve", wk[1][:, 0:n], wk[0][:, 0:n], oml, lbf, ALU.mult, ALU.add, [wk_t[0], c_t], [wk_t[1]])
                act(wk[1][:, 0:n], wk[1][:, 0:n], AF.Ln, [wk_t[1]], [wk_t[1]])
                ts("dve", kcf[:, 0:n], wk[0][:, 0:n], noml, oml, ALU.mult, ALU.add, [wk_t[0], c_t], [kcf_t])
                if ti > 0:
                    pn = tiles[ti - 1][1]
                    cp("dve", G[:, 0:1], G[:, pn:pn + 1], [G_t], [G_t])
                S.op("dve", lambda h, n=n: h.tensor_tensor_scan(G[:, 1:n + 1], onesb[:, 0:n],
                                                                 wk[1][:, 0:n], G[:, 0:1], ALU.mult, ALU.add),
                     [wk_t[1], G_t, c_t], [G_t])
                G3 = G[:, 1:n + 1].rearrange("p (j t) -> p j t", t=64)
                gmid = G[:, 32:32 + n].rearrange("p (j t) -> p j t", t=64)[:, :, 0:1].to_broadcast([128, nch, 64])
                tt("dve", wk[2][:, 0:n].rearrange("p (j t) -> p j t", t=64), G3, gmid, ALU.subtract, [G_t], [wk_t[2]])
                act(wk[3][:, 0:n], wk[2][:, 0:n], AF.Exp, [wk_t[2]], [wk_t[3]])
                act(wk[2][:, 0:n], wk[2][:, 0:n], AF.Exp, [wk_t[2]], [wk_t[2]], scale=-1.0)
                gs = G[:, 0:n].rearrange("p (j t) -> p j t", t=64)[:, :, 0]
                gm = G[:, 32:32 + n].rearrange("p (j t) -> p j t", t=64)[:, :, 0]
                ge = G[:, 64:64 + n].rearrange("p (j t) -> p j t", t=64)[:, :, 0]
                tt("dve", fac[:, 0, 0:nch], ge, gs, ALU.subtract, [G_t], [fac_t])
                tt("dve", fac[:, 1, 0:nch], ge, gm, ALU.subtract, [G_t], [fac_t])
                tt("dve", fac[:, 2, 0:nch], gm, gs, ALU.subtract, [G_t], [fac_t])
                act(fac[:, :, 0:nch], fac[:, :, 0:nch], AF.Exp, [fac_t], [fac_t])
                proj(wq, wqt, c0, n, B_PB)
                act(wk[0][:, 0:n], ps[B_PB][:, 0:n], AF.Silu, [ps_t[B_PB]], [wk_t[0]])
                tt("dve", qtb[:, 0:n], wk[0][:, 0:n], wk[3][:, 0:n], ALU.mult, [wk_t[0], wk_t[3]], [qtb_t])
                tt("dve", kcf[:, 0:n], kcf[:, 0:n], wk[2][:, 0:n], ALU.mult, [kcf_t, wk_t[2]], [kcf_t])
                cp("pool", ktb[:, 0:n], kcf[:, 0:n], [kcf_t], [ktb_t])
                proj(wi, wit, c0, n, B_PA)
                cp("act", wk[1][:, 0:n], ps[B_PA][:, 0:n], [ps_t[B_PA]], [wk_t[1]])
                nb4 = n // 128
                for j in range(nb4):
                    tr(ps[B_TR][:, j * 128:(j + 1) * 128], kcf[:, j * 128:(j + 1) * 128], [kcf_t], [ps_t[B_TR]])
                cp("dve", ktok[:, 0:nb4, :], ps[B_TR][:, 0:n].rearrange("p (j n) -> p j n", n=128), [ps_t[B_TR]], [ktok_t])
                for j in range(nb4):
                    tr(ps[B_TR][:, j * 128:(j + 1) * 128], wk[1][:, j * 128:(j + 1) * 128], [wk_t[1]], [ps_t[B_TR]])
                cp("act", vtok[:, 0:nb4, :], ps[B_TR][:, 0:n].rearrange("p (j n) -> p j n", n=128), [ps_t[B_TR]], [vtok_t])
                for j in range(nch):
                    gch = c0 // 64 + j
                    jb, jr = j // 2, (j % 2) * 64
                    is_smp = gch >= TP // 64
                    if is_smp:
                        s = gch - TP // 64
                        dma(Sst[:], shg[l, s, hh], [], [Sst_t], q="act")
                        Sprev, Sprev_t = Sst, Sst_t
                        Snew, Snew_t = Stmp, Stmp_t
                    else:
                        Sprev, Sprev_t = Sa[cur], Sa_t[cur]
                        Snew, Snew_t = Sa[1 - cur], Sa_t[1 - cur]
                    sb_i = sidx % 2
                    sidx += 1
                    ts("pool", Sbf[sb_i][:], Sprev[:], fac[:, 2, j:j + 1], None, ALU.mult, None, [Sprev_t, fac_t],
                       [Sbf_t[sb_i]])
                    pbk = B_X if j % 2 == 0 else B_Z1
                    abk = B_Z0 if j % 2 == 0 else B_PB
                    mm(ps[pbk][:, 0:128], ktok[jr:jr + 64, jb, :], vtok[jr:jr + 64, jb, :], True, True,
                       [ktok_t, vtok_t], [ps_t[pbk]])
                    ts("dve", Snew[:], Sprev[:], fac[:, 0, j:j + 1], None, ALU.mult, None, [Sprev_t, fac_t, Snew_t], [Snew_t])
                    stt("dve", Snew[:], ps[pbk][:, 0:128], fac[:, 1, j:j + 1], Snew[:], ALU.mult, ALU.add,
                        [ps_t[pbk], fac_t, Snew_t], [Snew_t])
                    mm(ps[abk][jr:jr + 64, 0:64], ktb[:, j * 64:(j + 1) * 64], qtb[:, j * 64:(j + 1) * 64], True, True,
                       [ktb_t, qtb_t], [ps_t[abk]])
                    tt("dve", attb[sb_i][jr:jr + 64, :], ps[abk][jr:jr + 64, 0:64], maskle2[jr:jr + 64, jr // 64, :], ALU.mult,
                       [ps_t[abk], c_t], [attb_t[sb_i]])
                    mm(ps[B_AV][:, j * 64:(j + 1) * 64], vtok[jr:jr + 64, jb, :], attb[sb_i][jr:jr + 64, :], j == 0, False,
                       [vtok_t, attb_t[sb_i], ps_t[B_AV]], [ps_t[B_AV]])
                    mm(ps[B_AV][:, j * 64:(j + 1) * 64], Sbf[sb_i][:], qtb[:, j * 64:(j + 1) * 64], False, j == nch - 1,
                       [Sbf_t[sb_i], qtb_t, ps_t[B_AV]], [ps_t[B_AV]])
                    if is_smp:
                        dma(hs[l, s, hh], Snew[:], [Snew_t], [], is_out=True)
                    else:
                        cur = 1 - cur
                        if gch == TP // 64 - 1:
                            dma(hp[l, hh], Snew[:], [Snew_t], [], is_out=True)
                cp("act", wk[0][:, 0:n], ps[B_AV][:, 0:n], [ps_t[B_AV]], [wk_t[0]])
                act(wk[1][:, 0:n], wk[0][:, 0:n], AF.Square, [wk_t[0]], [wk_t[1]])
                mm(ps[B_ST][:, 0:n], onesf[:], wk[1][:, 0:n], True, True, [wk_t[1], c_t], [ps_t[B_ST]])
                rstd_from(B_ST, n, 1.0 / 128.0, 4)
                stt("dve", wk[0][:, 0:n], wk[0][:, 0:n], vec[:, V_GH + l:V_GH + l + 1], wk[4][:, 0:n],
                    ALU.mult, ALU.mult, [wk_t[0], wk_t[4], c_t], [wk_t[0]])
                proj(wg, wgt, c0, n, B_PB)
                act(wk[5][:, 0:n], ps[B_PB][:, 0:n], AF.Silu, [ps_t[B_PB]], [wk_t[5]])
                tt("dve", yT[:, hh, c0:c0 + n], wk[0][:, 0:n], wk[5][:, 0:n], ALU.mult, [wk_t[0], wk_t[5]],
                   yw(hh, c0, n))
        return alltr

    def run_phase(fn, l):
        ph = contextlib.ExitStack()
        trks = fn(l, ph)
        last_bar[0] = memset("pool", bar[:], 0.0, trks)
        ph.close()

    for l in range(D):
        plan_layer(l)
    import os
    KSTOP = int(os.environ.get("KSTOP", "99"))
    for l in range(D):
        if KSTOP < 99 and l > 0:
            break
        if KSTOP >= 1:
            norm_phase(l)
        if KSTOP >= 2:
            run_phase(branch_a, l)
        if KSTOP >= 3:
            merge_phase(l, 0)
        if KSTOP >= 4:
            run_phase(branch_b, l)
        if KSTOP >= 5:
            merge_phase(l, 1)
        if KSTOP >= 6:
            run_phase(branch_c, l)
        if KSTOP >= 7:
            merge_phase(l, 2)

    for b in range(1, NB):
        si = b % 2
        ti = (b * 128) // 512
        for half in range(2):
            hb = wk[2 + half]
            hbt = wk_t[2 + half]
            dma(hb[:].rearrange("p (j n) -> p j n", j=4), hd[:, half * 4:half * 4 + 4, b * 128:(b + 1) * 128],
                [hd_t[c][ti] for c in range(half * 4, half * 4 + 4)], [hbt])
            bank = B_PA + half
            for j in range(4):
                tr(ps[bank][:, j * 128:(j + 1) * 128], hb[:, j * 128:(j + 1) * 128], [hbt], [ps_t[bank]])
            cp("act" if half else "dve", stg[si][:, half * 512:(half + 1) * 512], ps[bank][:], [ps_t[bank]], [stg_t[si]])
        if b < NBP:
            dma(yp[(b - 1) * 128:b * 128, :], stg[si][:], [stg_t[si]], [], is_out=True)
        else:
            dma(ys[(b - NBP) * 128:(b - NBP + 1) * 128, :], stg[si][:], [stg_t[si]], [], is_out=True)

    S.emit(st)
    st.close()
    return nc


def host_prep(cfg, inputs):
    D = cfg.DEPTH
    f = lambda a: np.ascontiguousarray(np.asarray(a, dtype=np.float32))

    def pc(v):
        return np.asarray(v, np.float32).reshape(D, 8, 128).transpose(2, 0, 1).reshape(128, D * 8)
    vecs = np.zeros((128, NVEC), np.float32)
    vecs[:, 0:8 * D] = pc(inputs["norm_g"])
    vecs[:, 8 * D:16 * D] = pc(inputs["pool_scale"])
    vecs[:, 16 * D:24 * D] = pc(inputs["hgrn_lower_bounds"])
    vecs[:, 24 * D:25 * D] = np.asarray(inputs["q_norm_g"], np.float32).T
    vecs[:, 25 * D:26 * D] = np.asarray(inputs["k_norm_g"], np.float32).T
    vecs[:, 26 * D:27 * D] = np.asarray(inputs["hgrn_norm_g"], np.float32).T
    consts = np.zeros((128, 64), np.float32)
    for g, w in enumerate((2, 4, 8, 16)):
        consts[:, g * 16:(g + 1) * 16] = 1.0 / np.minimum(np.arange(16) + 1.0, float(w))
    shared = dict(meta=f(inputs["meta_tokens"]), vecs=vecs, consts=consts, w_in=f(inputs["w_in"]),
                  w_pool=f(inputs["w_pool"]), w_branch=f(inputs["w_branch"]), w_out=f(inputs["w_out"]))
    maps = []
    for c in range(8):
        m = dict(shared)
        m["xp"] = f(inputs["x_prompt"][c])
        m["xs"] = f(np.asarray(inputs["x_sample"][4 * c:4 * c + 4]).reshape(256, 1024))
        m["ck"] = f(np.asarray(inputs["cache_k"])[:, 4 * c:4 * c + 4])
        m["cv"] = f(np.asarray(inputs["cache_v"])[:, 4 * c:4 * c + 4])
        m["spool"] = f(np.asarray(inputs["state_pool"])[:, 4 * c:4 * c + 4])
        m["shg"] = f(np.asarray(inputs["state_hgrn"])[:, 4 * c:4 * c + 4])
        maps.append(m)
    return maps


def gather(cfg, res):
    D = cfg.DEPTH
    r = res
    cat = lambda k, ax: np.concatenate([x[k] for x in r], axis=ax)
    y_prompt = np.stack([x["yp"] for x in r], 0)
    y_sample = np.concatenate([x["ys"].reshape(4, 64, 1024) for x in r], 0)
    k_prompt = np.stack([x["kp"].reshape(D, cfg.LP, 8, 128) for x in r], 1)
    v_prompt = np.stack([x["vp"].reshape(D, cfg.LP, 8, 128) for x in r], 1)
    pool_prompt = np.stack([x["pp"] for x in r], 1)
    hgrn_prompt = np.stack([x["hp"] for x in r], 1)
    k_sample = np.concatenate([x["ks"].reshape(D, 4, 64, 8, 128) for x in r], 1)
    v_sample = np.concatenate([x["vs"].reshape(D, 4, 64, 8, 128) for x in r], 1)
    pool_sample = np.concatenate([x["pls"] for x in r], 1)
    hgrn_sample = np.concatenate([x["hs"] for x in r], 1)
    outs = (y_prompt, y_sample, k_prompt, v_prompt, pool_prompt, hgrn_prompt, k_sample, v_sample, pool_sample,
            hgrn_sample)
    return tuple(np.ascontiguousarray(o, dtype=np.float32) for o in outs)


def kernel(**inputs):
    seq = int(np.asarray(inputs["x_prompt"]).shape[1])
    past = int(np.asarray(inputs["cache_k"]).shape[2])
    depth = int(np.asarray(inputs["w_in"]).shape[0])
    cfg = Cfg(seq, past, depth)
    nc = build(cfg)
    maps = host_prep(cfg, inputs)
    res = run_bass_kernel_spmd(nc, maps, core_ids=list(range(8)))
    return gather(cfg, res.results)
```

```python
import contextlib
import os
import numpy as np
import concourse.bass as bass
import concourse.mybir as mybir
from concourse.bass_utils import run_bass_kernel_spmd

F32 = mybir.dt.float32
BF16 = mybir.dt.bfloat16
ALU = mybir.AluOpType
AF = mybir.ActivationFunctionType

NDMA_SEMS = 8
P = 128
EPS = 1e-6


class Trk:
    __slots__ = ("w", "r")

    def __init__(self):
        self.w = None
        self.r = []


class Op:
    __slots__ = ("eng", "fn", "deps", "is_dma", "needs_inc", "token", "dma_slot")

    def __init__(self, eng, fn, deps, is_dma):
        self.eng = eng
        self.fn = fn
        self.deps = deps
        self.is_dma = is_dma
        self.needs_inc = False
        self.token = None
        self.dma_slot = None


class Sched:
    ENGS = ("pe", "act", "dve", "pool", "sp")

    def __init__(self, nc):
        self.nc = nc
        self.q = {e: [] for e in self.ENGS}
        self.ndma = {e: 0 for e in self.ENGS}
        self.out_dmas = []

    def op(self, eng, fn, reads=(), writes=(), is_dma=False, is_out=False):
        deps = []
        seen = set()

        def add(o):
            if o is not None and id(o) not in seen:
                seen.add(id(o))
                deps.append(o)
        for t in reads:
            add(t.w)
        for t in writes:
            add(t.w)
            for r in t.r:
                add(r)
        o = Op(eng, fn, deps, is_dma)
        if is_dma:
            o.dma_slot = self.ndma[eng]
            self.ndma[eng] += 1
        for t in reads:
            if not is_dma:
                t.r = [r for r in t.r if r.is_dma or r.eng != eng]
            t.r.append(o)
        for t in writes:
            t.w = o
            t.r = []
        self.q[eng].append(o)
        if is_out:
            self.out_dmas.append(o)
        return o

    def emit(self, st):
        nc = self.nc
        esem = {e: st.enter_context(nc.semaphore("s_" + e)) for e in self.ENGS}
        dsem = {e: [st.enter_context(nc.semaphore("d_%s%d" % (e, i))) for i in range(NDMA_SEMS)]
                for e in self.ENGS if self.ndma[e] > 0}
        for e in self.ENGS:
            for o in self.q[e]:
                for d in o.deps:
                    if d.is_dma:
                        continue
                    if d.eng == "pe" and o.eng == "pe" and not o.is_dma:
                        continue
                    d.needs_inc = True
        for e in self.ENGS:
            c = 0
            for o in self.q[e]:
                if o.is_dma:
                    s = dsem[e][o.dma_slot % NDMA_SEMS]
                    o.token = (s, 16 * (o.dma_slot // NDMA_SEMS + 1))
                elif o.needs_inc:
                    c += 1
                    o.token = (esem[e], c)
        final = {}
        for e in self.ENGS:
            for o in self.q[e]:
                if o.is_dma:
                    final[id(o.token[0])] = o.token
        block = st.enter_context(nc.Block())

        def run_queue(e, h):
            waited = {}

            def wait(tok):
                s, v = tok
                k = id(s)
                if waited.get(k, 0) < v:
                    h.wait_ge(s, v)
                    waited[k] = v
            for o in self.q[e]:
                for d in o.deps:
                    if (not d.is_dma) and d.eng == "pe" and e == "pe" and not o.is_dma:
                        continue
                    wait(d.token)
                if o.is_dma and o.dma_slot >= NDMA_SEMS:
                    s, v = o.token
                    wait((s, v - 16))
                ins = o.fn(h)
                if o.is_dma:
                    ins.then_inc(o.token[0], 16)
                elif o.needs_inc:
                    ins.then_inc(o.token[0], 1)
            if e == "sp":
                for tok in final.values():
                    wait(tok)

        @block.tensor
        def _(h):
            run_queue("pe", h)

        @block.scalar
        def _(h):
            run_queue("act", h)

        @block.vector
        def _(h):
            run_queue("dve", h)

        @block.gpsimd
        def _(h):
            run_queue("pool", h)

        @block.sync
        def _(h):
            run_queue("sp", h)


class Cfg:
    def __init__(self, seq=2048, past=2048, depth=2):
        self.SEQ = seq
        self.PAST = past
        self.DEPTH = depth
        self.NBP = 1 + seq // 128
        self.NB = self.NBP + 2
        self.T = self.NB * 128
        self.TP = self.NBP * 128
        self.LP = 16 + seq
        self.tiles = [(s, min(512, self.T - s)) for s in range(0, self.T, 512)]
        self.NPC = past // 512


NVEC = 64


def build(cfg):
    nc = bass.Bass("TRN2", target_bir_lowering=False)
    D = cfg.DEPTH
    T, TP, NB, NBP = cfg.T, cfg.TP, cfg.NB, cfg.NBP
    tiles = cfg.tiles
    NT = len(tiles)

    def din(name, shape):
        return nc.dram_tensor(name, shape, F32, kind="ExternalInput").ap()

    def dout(name, shape):
        return nc.dram_tensor(name, shape, F32, kind="ExternalOutput").ap()

    xp = din("xp", [cfg.SEQ, 1024])
    xs = din("xs", [256, 1024])
    ck = din("ck", [D, 4, cfg.PAST, 8, 128])
    cv = din("cv", [D, 4, cfg.PAST, 8, 128])
    spool = din("spool", [D, 4, 15, 1024])
    shg = din("shg", [D, 4, 8, 128, 128])
    meta = din("meta", [16, 1024])
    vecs = din("vecs", [128, NVEC])
    consts = din("consts", [128, 64])
    w_in = din("w_in", [D, 1024, 13 * 1024])
    w_pool = din("w_pool", [D, 4, 256, 256])
    w_branch = din("w_branch", [D, 3, 1024, 1024])
    w_out = din("w_out", [D, 1024, 1024])

    yp = dout("yp", [cfg.SEQ, 1024])
    ys = dout("ys", [256, 1024])
    kp = dout("kp", [D, cfg.LP, 1024])
    vp = dout("vp", [D, cfg.LP, 1024])
    pp = dout("pp", [D, 15, 1024])
    hp = dout("hp", [D, 8, 128, 128])
    ks = dout("ks", [D, 256, 1024])
    vs = dout("vs", [D, 256, 1024])
    pls = dout("pls", [D, 4, 15, 1024])
    hs = dout("hs", [D, 4, 8, 128, 128])
    hd = nc.dram_tensor("hscr", [128, 8, T], F32).ap()

    S = Sched(nc)
    st = contextlib.ExitStack()

    def sb(name, shape, dt):
        return st.enter_context(nc.sbuf_tensor(name, shape, dt))

    def psb(name):
        return st.enter_context(nc.psum_tensor(name, [128, 512], F32))

    xnT = st.enter_context(nc.sbuf_tensor("xnT", [128, 8, T], BF16, side="right"))
    yT = st.enter_context(nc.sbuf_tensor("yT", [128, 8, T], BF16, side="right"))
    tmpT = st.enter_context(nc.sbuf_tensor("tmpT", [128, 8, T], BF16, side="right"))
    xn_t = [Trk() for _ in range(NB)]
    y_t = [[Trk() for _ in range(NB)] for _ in range(8)]
    tmp_t = [[Trk() for _ in range(NT)] for _ in range(8)]
    hd_t = [[Trk() for _ in range(NT)] for _ in range(8)]
    NWS = 8
    wbf = [sb("wbf%d" % i, [128, 8, 128], BF16) for i in range(NWS)]
    wbf_t = [Trk() for _ in range(NWS)]
    wst = [sb("wst%d" % i, [128, 8, 128], F32) for i in range(2)]
    wst_t = [Trk() for _ in range(2)]
    stg = [sb("stg%d" % i, [128, 1024], F32) for i in range(2)]
    stg_t = [Trk() for _ in range(2)]
    NWK = 8
    wk = [sb("wk%d" % i, [128, 512], F32) for i in range(NWK)]
    wk_t = [Trk() for _ in range(NWK)]
    ident = sb("ident", [128, 128], F32)
    onesf = sb("onesf", [128, 128], F32)
    negtri = sb("negtri", [128, 128], BF16)
    negones = sb("negones", [128, 128], BF16)
    mask01 = sb("mask01", [128, 128], BF16)
    maskle2 = sb("maskle2", [128, 2, 64], F32)
    onesb = sb("onesb", [128, 512], BF16)
    vec = sb("vec", [128, NVEC], F32)
    cst = sb("cst", [128, 64], F32)
    lbv = sb("lbv", [128, 4, D * 8], F32)
    gqs = sb("gqs", [128, D], F32)
    c_t = Trk()
    ps = [psb("ps%d" % i) for i in range(8)]
    ps_t = [Trk() for _ in range(8)]
    B_PA, B_PB, B_ST, B_TR, B_Z0, B_Z1, B_AV, B_X = range(8)

    def blk(c0, n):
        return range(c0 // 128, (c0 + n + 127) // 128)

    def xnr(c0, n):
        return [xn_t[b] for b in blk(c0, n)]

    def yw(c, c0, n):
        return [y_t[c][b] for b in blk(c0, n)]

    def yr(c0, n):
        return [y_t[c][b] for c in range(8) for b in blk(c0, n)]

    def dma(out, in_, reads=(), writes=(), is_out=False, q="sp"):
        return S.op(q, lambda h: h.dma_start(out=out, in_=in_), reads, writes, is_dma=True, is_out=is_out)

    def mm(out, lhsT, rhs, start, stop, reads, writes, sgc=False):
        return S.op("pe", lambda h: h.matmul(out, lhsT=lhsT, rhs=rhs, start=start, stop=stop,
                                             skip_group_check=sgc), reads, writes)

    def tr(out, in_, reads, writes):
        k = in_.shape[0]
        return S.op("pe", lambda h: h.transpose(out, in_, ident[0:k, 0:k]), list(reads) + [c_t], writes)

    def act(out, in_, func, reads, writes, bias=None, scale=None):
        kw = {}
        if bias is not None:
            kw["bias"] = bias
        if scale is not None:
            kw["scale"] = scale
        return S.op("act", lambda h: h.activation(out, in_, func, **kw), reads, writes)

    def tt(eng, out, a, b, op, reads, writes):
        return S.op(eng, lambda h: h.tensor_tensor(out, a, b, op), reads, writes)

    def ts(eng, out, a, s1, s2, op0, op1, reads, writes):
        if op1 is None:
            return S.op(eng, lambda h: h.tensor_scalar(out, a, s1, None, op0), reads, writes)
        return S.op(eng, lambda h: h.tensor_scalar(out, a, s1, s2, op0, op1), reads, writes)

    def stt(eng, out, a, s, b, op0, op1, reads, writes):
        return S.op(eng, lambda h: h.scalar_tensor_tensor(out, a, s, b, op0, op1), reads, writes)

    def cp(eng, out, in_, reads, writes):
        if eng == "act":
            return S.op("act", lambda h: h.copy(out, in_), reads, writes)
        return S.op(eng, lambda h: h.tensor_copy(out, in_), reads, writes)

    def memset(eng, ap, v, writes):
        return S.op(eng, lambda h: h.memset(ap, v), (), writes)

    class WStream:
        def __init__(self):
            self.plan = []
            self.issued = 0
            self.taken = 0
            self.LOOK = 3

        def add(self, src, nk=8):
            self.plan.append((src, nk))

        def _issue(self):
            i = self.issued
            src, nk = self.plan[i]
            s_i = i % 2
            b_i = i % NWS
            dma(wst[s_i][:, 0:nk, :], src, writes=[wst_t[s_i]])
            cp("pool", wbf[b_i][:, 0:nk, :], wst[s_i][:, 0:nk, :], [wst_t[s_i]], [wbf_t[b_i]])
            self.issued += 1

        def get(self):
            i = self.taken
            while self.issued < len(self.plan) and self.issued <= i + self.LOOK:
                self._issue()
            self.taken += 1
            return wbf[i % NWS], wbf_t[i % NWS]

    W = WStream()

    def wsrc_in(l, grp, c):
        col = grp * 1024 + c * 128
        return w_in[l].rearrange("(k p) n -> p k n", p=128)[:, :, col:col + 128]

    def wsrc_br(l, n, c):
        return w_branch[l, n].rearrange("(k p) n -> p k n", p=128)[:, :, c * 128:(c + 1) * 128]

    def wsrc_out(l, c):
        return w_out[l].rearrange("(k p) n -> p k n", p=128)[:, :, c * 128:(c + 1) * 128]

    def wsrc_pool(l, g, e):
        return w_pool[l, g].rearrange("(k p) n -> p k n", p=128)[:, :, e * 128:(e + 1) * 128]

    G_UA, G_GA, G_QB, G_KB, G_VB, G_GB, G_FC, G_QC, G_IC, G_GC, G_MA, G_MB, G_MC = range(13)

    def plan_layer(l):
        for g in range(4):
            W.add(wsrc_in(l, G_UA, 2 * g))
            W.add(wsrc_in(l, G_UA, 2 * g + 1))
            for e in range(2):
                W.add(wsrc_pool(l, g, e), 2)
                W.add(wsrc_in(l, G_GA, 2 * g + e))
        plan_merge(l, 0, G_MA)
        for hh in range(8):
            W.add(wsrc_in(l, G_KB, hh))
            W.add(wsrc_in(l, G_VB, hh))
            W.add(wsrc_in(l, G_QB, hh))
            W.add(wsrc_in(l, G_GB, hh))
        plan_merge(l, 1, G_MB)
        for hh in range(8):
            W.add(wsrc_in(l, G_FC, hh))
            W.add(wsrc_in(l, G_QC, hh))
            W.add(wsrc_in(l, G_IC, hh))
            W.add(wsrc_in(l, G_GC, hh))
        plan_merge(l, 2, G_MC)

    def plan_merge(l, n, gm):
        for c in range(8):
            W.add(wsrc_br(l, n, c))
            W.add(wsrc_in(l, gm, c))
        for c in range(8):
            W.add(wsrc_out(l, c))

    def proj(wslot, wt, c0, n, bank, nk=8, src=None, src_reads=None):
        srcT = xnT if src is None else src
        rd = xnr(c0, n) if src_reads is None else src_reads
        for k in range(nk):
            mm(ps[bank][:, 0:n], wslot[:, k, :], srcT[:, k, c0:c0 + n], k == 0, k == nk - 1,
               [wt] + list(rd), [ps_t[bank]])

    def rstd_from(bank, n, scale, out_wk):
        act(wk[out_wk][:, 0:n], ps[bank][:, 0:n], AF.Ln, [ps_t[bank]], [wk_t[out_wk]], bias=EPS, scale=scale)
        act(wk[out_wk][:, 0:n], wk[out_wk][:, 0:n], AF.Exp, [wk_t[out_wk]], [wk_t[out_wk]], scale=-0.5)

    dma(vec[:], vecs[:, :], writes=[c_t])
    dma(cst[:], consts[:, :], writes=[c_t])
    memset("pool", onesf[:], 1.0, [c_t])
    memset("pool", onesb[:], 1.0, [c_t])
    memset("pool", negones[:], -1.0, [c_t])
    S.op("pool", lambda h: h.affine_select(ident[:], onesf[:], pattern=[[-1, 128]], compare_op=ALU.is_equal,
                                           fill=0.0, base=0, channel_multiplier=1), [c_t], [c_t])
    S.op("pool", lambda h: h.affine_select(negtri[:], negones[:], pattern=[[-1, 128]], compare_op=ALU.is_ge,
                                           fill=0.0, base=0, channel_multiplier=1), [c_t], [c_t])
    S.op("pool", lambda h: h.affine_select(mask01[:], onesb[:, 0:128], pattern=[[1, 128]], compare_op=ALU.is_gt,
                                           fill=0.0, base=0, channel_multiplier=-1), [c_t], [c_t])
    S.op("pool", lambda h: h.affine_select(maskle2[:], onesf[:].rearrange("p (a b) -> p a b", a=2),
                                           pattern=[[64, 2], [1, 64]], compare_op=ALU.is_ge, fill=0.0, base=0,
                                           channel_multiplier=-1), [c_t], [c_t])
    V_NG, V_PS, V_LB, V_GQ = 0, 8 * D, 16 * D, 24 * D
    V_GK, V_GH = V_GQ + D, V_GQ + 2 * D
    lbraw = vec[:, V_LB:V_LB + 8 * D].rearrange("p (l c) -> p l c", l=D)
    mx = wk[0][:, 0:8]
    ex = wk[0][:, 8:8 + 8 * D].rearrange("p (l c) -> p l c", l=D)
    sm = wk[0][:, 200:208]
    cp("dve", mx, lbraw[:, 0, :], [c_t], [wk_t[0]])
    for l in range(1, D):
        tt("dve", mx, mx, lbraw[:, l, :], ALU.max, [c_t, wk_t[0]], [wk_t[0]])
    for l in range(D):
        tt("dve", ex[:, l, :], lbraw[:, l, :], mx, ALU.subtract, [c_t, wk_t[0]], [wk_t[0]])
    act(wk[0][:, 8:8 + 8 * D], wk[0][:, 8:8 + 8 * D], AF.Exp, [wk_t[0]], [wk_t[0]])
    cp("dve", sm, ex[:, 0, :], [wk_t[0]], [wk_t[0]])
    for l in range(1, D):
        tt("dve", sm, sm, ex[:, l, :], ALU.add, [wk_t[0]], [wk_t[0]])
    S.op("dve", lambda h: h.reciprocal(sm, sm), [wk_t[0]], [wk_t[0]])
    for l in range(D):
        tt("dve", ex[:, l, :], ex[:, l, :], sm, ALU.mult, [wk_t[0]], [wk_t[0]])
    lb4 = lbv[:].rearrange("p f (l c) -> p f l c", l=D)
    memset("dve", lbv[:, 0, 0:8], 0.0, [c_t])
    for l in range(1, D):
        tt("dve", lb4[:, 0, l, :], lb4[:, 0, l - 1, :], ex[:, l, :], ALU.add, [wk_t[0], c_t], [c_t])
    ts("dve", lbv[:, 1, :], lbv[:, 0, :], -1.0, 1.0, ALU.mult, ALU.add, [c_t], [c_t])
    ts("dve", lbv[:, 2, :], lbv[:, 0, :], 1e-30, None, ALU.max, None, [c_t], [c_t])
    ts("dve", lbv[:, 3, :], lbv[:, 1, :], -1.0, None, ALU.mult, None, [c_t], [c_t])
    ts("dve", gqs[:], vec[:, V_GQ:V_GQ + D], float(128 ** -0.5), None, ALU.mult, None, [c_t], [c_t])

    def store_h_block(b, si):
        for half in range(2):
            bank = B_PA + half
            for j in range(4):
                c = half * 4 + j
                tr(ps[bank][:, j * 128:(j + 1) * 128], stg[si][:, c * 128:(c + 1) * 128], [stg_t[si]], [ps_t[bank]])
            o = wk[half]
            cp("act" if half else "dve", o[:], ps[bank][:], [ps_t[bank]], [wk_t[half]])
            ti = (b * 128) // 512
            dma(hd[:, half * 4:half * 4 + 4, b * 128:(b + 1) * 128],
                o[:].rearrange("p (j n) -> p j n", j=4), [wk_t[half]], [hd_t[c][ti] for c in range(half * 4, half * 4 + 4)])

    for b in range(NB):
        si = b % 2
        if b == 0:
            memset("pool", stg[si][:], 0.0, [stg_t[si]])
            dma(stg[si][112:128, :], meta[:, :], writes=[stg_t[si]])
        elif b < NBP:
            dma(stg[si][:], xp[(b - 1) * 128:b * 128, :], writes=[stg_t[si]])
        else:
            dma(stg[si][:], xs[(b - NBP) * 128:(b - NBP + 1) * 128, :], writes=[stg_t[si]])
        store_h_block(b, si)

    def norm_phase(l):
        for ti, (c0, n) in enumerate(tiles):
            for c in range(8):
                hb = wk[2 + (c % 2)]
                hbt = wk_t[2 + (c % 2)]
                dma(hb[:, 0:n], hd[:, c, c0:c0 + n], [hd_t[c][ti]], [hbt])
                act(hb[:, 0:n], hb[:, 0:n], AF.Square, [hbt], [hbt])
                mm(ps[B_ST][:, 0:n], onesf[:], hb[:, 0:n], c == 0, c == 7, [hbt, c_t], [ps_t[B_ST]])
            rstd_from(B_ST, n, 1.0 / 1024.0, 4)
            for c in range(8):
                hb = wk[5 + (c % 2)]
                hbt = wk_t[5 + (c % 2)]
                dma(hb[:, 0:n], hd[:, c, c0:c0 + n], [hd_t[c][ti]], [hbt])
                stt("dve", xnT[:, c, c0:c0 + n], hb[:, 0:n], vec[:, V_NG + l * 8 + c:V_NG + l * 8 + c + 1],
                    wk[4][:, 0:n], ALU.mult, ALU.mult, [hbt, wk_t[4], c_t], xnr(c0, n))

    def merge_phase(l, nbr):
        pairs = [(B_PA, B_PB), (B_Z0, B_Z1), (B_AV, B_X), (B_ST, B_TR)]
        wks = [0, 1, 6, 7]
        it = 0
        for c in range(8):
            wb, wbt = W.get()
            wm, wmt = W.get()
            for ti, (c0, n) in enumerate(tiles):
                pa, pb = pairs[it % 4]
                wi_ = wks[it % 4]
                it += 1
                proj(wb, wbt, c0, n, pa, src=yT, src_reads=yr(c0, n))
                proj(wm, wmt, c0, n, pb)
                act(wk[wi_][:, 0:n], ps[pb][:, 0:n], AF.Sigmoid, [ps_t[pb]], [wk_t[wi_]])
                tt("dve", tmpT[:, c, c0:c0 + n], ps[pa][:, 0:n], wk[wi_][:, 0:n], ALU.mult,
                   [ps_t[pa], wk_t[wi_]], [tmp_t[c][ti]])
        banks = [B_PA, B_PB, B_Z0, B_Z1]
        it = 0
        for c in range(8):
            wo, wot = W.get()
            for ti, (c0, n) in enumerate(tiles):
                bank = banks[it % 4]
                hb = wk[2 + (it % 4)]
                hbt = wk_t[2 + (it % 4)]
                it += 1
                dma(hb[:, 0:n], hd[:, c, c0:c0 + n], [hd_t[c][ti]], [hbt])
                for k in range(8):
                    mm(ps[bank][:, 0:n], wo[:, k, :], tmpT[:, k, c0:c0 + n], k == 0, k == 7,
                       [wot, tmp_t[k][ti]], [ps_t[bank]])
                tt("dve", hb[:, 0:n], hb[:, 0:n], ps[bank][:, 0:n], ALU.add, [hbt, ps_t[bank]], [hbt])
                dma(hd[:, c, c0:c0 + n], hb[:, 0:n], [hbt], [hd_t[c][ti]])

    bar = sb("bar", [128, 8], F32)
    last_bar = [None]

    def PT():
        t = Trk()
        t.w = last_bar[0]
        return t

    def branch_a(l, ph):
        U = ph.enter_context(nc.sbuf_tensor("pa_U_%d" % l, [128, T], F32))
        X = ph.enter_context(nc.sbuf_tensor("pa_X_%d" % l, [128, T], F32))
        Y = ph.enter_context(nc.sbuf_tensor("pa_Y_%d" % l, [128, T], F32))
        Dd = tmpT
        us = ph.enter_context(nc.sbuf_tensor("pa_us_%d" % l, [128, 4, 80], F32))
        ux = ph.enter_context(nc.sbuf_tensor("pa_ux_%d" % l, [128, 4, 80], F32))
        uy = ph.enter_context(nc.sbuf_tensor("pa_uy_%d" % l, [128, 4, 80], F32))
        ppre = ph.enter_context(nc.sbuf_tensor("pa_pre_%d" % l, [128, 4, 8, 16], F32))
        pst = ph.enter_context(nc.sbuf_tensor("pa_pst_%d" % l, [16, 5, 128], F32))
        U_t, X_t, Y_t, us_t, ux_t, uy_t, pre_t, pst_t = [PT() for _ in range(8)]
        D_t = [list(tmp_t[0]), list(tmp_t[1])]
        for s in range(4):
            si = s % 2
            dma(stg[si][0:15, :], spool[l, s], writes=[stg_t[si]])
            for c in range(8):
                tr(ps[B_TR][:, c * 16:c * 16 + 15], stg[si][0:15, c * 128:(c + 1) * 128], [stg_t[si]], [ps_t[B_TR]])
            cp("dve", ppre[:, s, :, 0:15], ps[B_TR][:, 0:128].rearrange("p (c r) -> p c r", r=16)[:, :, 0:15],
               [ps_t[B_TR]], [pre_t])
        for g in range(4):
            w = 2 << g
            nlev = g + 1
            for j in range(2):
                c = 2 * g + j
                wu, wut = W.get()
                for ti, (c0, n) in enumerate(tiles):
                    bank = B_PA + (ti % 2)
                    proj(wu, wut, c0, n, bank)
                    cp("act", U[:, c0:c0 + n], ps[bank][:, 0:n], [ps_t[bank]], [U_t])
                segs = [TP - 15] + [TP + 64 * s + 49 for s in range(4)]
                for i, s0 in enumerate(segs[0:4]):
                    tr(ps[B_TR][0:15, i * 128:(i + 1) * 128], U[:, s0:s0 + 15], [U_t], [ps_t[B_TR]])
                cp("dve", pst[0:15, 0:4, :], ps[B_TR][0:15, 0:512].rearrange("p (i n) -> p i n", i=4),
                   [ps_t[B_TR]], [pst_t])
                tr(ps[B_TR][0:15, 0:128], U[:, segs[4]:segs[4] + 15], [U_t], [ps_t[B_TR]])
                cp("dve", pst[0:15, 4, :], ps[B_TR][0:15, 0:128], [ps_t[B_TR]], [pst_t])
                dma(pp[l, :, c * 128:(c + 1) * 128], pst[0:15, 0, :], [pst_t], [], is_out=True)
                dma(pls[l, :, :, c * 128:(c + 1) * 128].rearrange("s r n -> r s n"), pst[0:15, 1:5, :], [pst_t], [],
                    is_out=True)
                cp("pool", us[:, :, 0:15], ppre[:, :, c, 0:15], [pre_t], [us_t])
                cp("pool", us[:, :, 15:79], U[:, TP:T].rearrange("p (s n) -> p s n", s=4), [U_t], [us_t])
                src, srct = U, U_t
                ssrc, ssrct = us, us_t
                bufs = [(X, X_t), (Y, Y_t)]
                sbufs = [(ux, ux_t), (uy, uy_t)]
                sh = 1
                for lev in range(nlev):
                    lo = 2 * sh - 1
                    dst, dstt = bufs[lev % 2]
                    tt("dve", dst[:, lo:T], src[:, lo:T], src[:, lo - sh:T - sh], ALU.add, [srct], [dstt])
                    sdst, sdstt = sbufs[lev % 2]
                    tt("pool", sdst[:, :, lo:79], ssrc[:, :, lo:79], ssrc[:, :, lo - sh:79 - sh], ALU.add,
                       [ssrct], [sdstt])
                    src, srct = dst, dstt
                    ssrc, ssrct = sdst, sdstt
                    sh *= 2
                memset("pool", Dd[:, j, 0:16], 0.0, D_t[j])
                stt("dve", Dd[:, j, 16:TP], src[:, 16:TP], 1.0 / w, U[:, 16:TP], ALU.mult, ALU.subtract,
                    [srct, U_t], D_t[j])
                tt("dve", X[:, 0:16] if src is not X else Y[:, 0:16], src[:, 112:128], cst[:, g * 16:g * 16 + 16],
                   ALU.mult, [srct, c_t], [X_t if src is not X else Y_t])
                fixb = X if src is not X else Y
                fixt = X_t if src is not X else Y_t
                tt("dve", Dd[:, j, 112:128], fixb[:, 0:16], U[:, 112:128], ALU.subtract, [fixt, U_t], D_t[j])
                stt("dve", Dd[:, j, TP:T].rearrange("p (s n) -> p s n", s=4), ssrc[:, :, 15:79], 1.0 / w,
                    us[:, :, 15:79], ALU.mult, ALU.subtract, [ssrct, us_t], D_t[j])
            for e in range(2):
                co = 2 * g + e
                wp_, wpt = W.get()
                wg, wgt = W.get()
                for ti, (c0, n) in enumerate(tiles):
                    pa, pb = [(B_PA, B_PB), (B_Z0, B_Z1), (B_AV, B_X)][ti % 3]
                    wi_ = [0, 1, 6][ti % 3]
                    for k in range(2):
                        mm(ps[pa][:, 0:n], wp_[:, k, :], Dd[:, k, c0:c0 + n], k == 0, k == 1,
                           [wpt] + D_t[k], [ps_t[pa]])
                    proj(wg, wgt, c0, n, pb)
                    act(wk[wi_][:, 0:n], ps[pb][:, 0:n], AF.Silu, [ps_t[pb]], [wk_t[wi_]])
                    stt("dve", yT[:, co, c0:c0 + n], ps[pa][:, 0:n], vec[:, V_PS + l * 8 + co:V_PS + l * 8 + co + 1],
                        wk[wi_][:, 0:n], ALU.mult, ALU.mult, [ps_t[pa], wk_t[wi_], c_t], yw(co, c0, n))
        return [U_t, X_t, Y_t, us_t, ux_t, uy_t, pre_t, pst_t]

    def branch_b(l, ph):
        def a(name, shape, dt):
            return ph.enter_context(nc.sbuf_tensor(name + "_%d" % l, shape, dt))
        knT = a("pb_knT", [128, T], BF16)
        Vtok = a("pb_Vtok", [128, NBP, 128], BF16)
        Vs = a("pb_Vs", [64, 4, 128], BF16)
        qn = a("pb_qn", [128, 512], BF16)
        sp = [a("pb_sp%d" % i, [128, 512], BF16) for i in range(2)]
        wT = [a("pb_wT%d" % i, [128, 512], BF16) for i in range(2)]
        ew = [a("pb_ew%d" % i, [128, 512], F32) for i in range(2)]
        sacc = a("pb_sacc", [128, 512], BF16)
        cs = a("pb_cs", [128, 4, 64], BF16)
        kst = [a("pb_kst%d" % i, [128, 4, 128], F32) for i in range(2)]
        vst = [a("pb_vst%d" % i, [128, 4, 128], F32) for i in range(2)]
        kcT = [a("pb_kcT%d" % i, [128, 512], BF16) for i in range(2)]
        vc = [a("pb_vc%d" % i, [128, 4, 128], BF16) for i in range(2)]
        knT_t, V_t, Vs_t, qn_t, sacc_t, cs_t = [PT() for _ in range(6)]
        sp_t = [PT(), PT()]
        wT_t = [PT(), PT()]
        ew_t = [PT(), PT()]
        kst_t = [PT(), PT()]
        vst_t = [PT(), PT()]
        kcT_t = [PT(), PT()]
        vc_t = [PT(), PT()]
        alltr = [knT_t, V_t, Vs_t, qn_t, sacc_t, cs_t] + sp_t + wT_t + ew_t + kst_t + vst_t + kcT_t + vc_t
        stgi = [0]

        def out_rows(dst_p, dst_s, l, hh, srcbuf, srct, c0, n):
            pass

        for hh in range(int(os.environ.get('KH', '8'))):
            KV = int(os.environ.get('KV', '9'))
            for which in range(2):
                wslot, wt_ = W.get()
                dst_p, dst_s = (kp, ks) if which == 0 else (vp, vs)
                for ti, (c0, n) in (enumerate(tiles) if KV >= 1 else []):
                    bank = B_PA + (ti % 2)
                    proj(wslot, wt_, c0, n, bank)
                    xi_ = 0 if ti % 2 == 0 else 6
                    qi_ = 1 if ti % 2 == 0 else 7
                    xf, xft = wk[xi_], wk_t[xi_]
                    cp("act", xf[:, 0:n], ps[bank][:, 0:n], [ps_t[bank]], [xft])
                    if which == 0:
                        act(wk[qi_][:, 0:n], xf[:, 0:n], AF.Square, [xft], [wk_t[qi_]])
                        mm(ps[B_ST][:, 0:n], onesf[:], wk[qi_][:, 0:n], True, True, [wk_t[qi_], c_t], [ps_t[B_ST]])
                        rstd_from(B_ST, n, 1.0 / 128.0, 4)
                        stt("dve", xf[:, 0:n], xf[:, 0:n], vec[:, V_GK + l:V_GK + l + 1], wk[4][:, 0:n],
                            ALU.mult, ALU.mult, [xft, wk_t[4], c_t], [xft])
                        cp("pool", knT[:, c0:c0 + n], xf[:, 0:n], [xft], [knT_t])
                    for b in (blk(c0, n) if (KV >= 2 and int(os.environ.get('KW', which)) == which) else []):
                        o0 = b * 128 - c0
                        si = stgi[0] % 2
                        stgi[0] += 1
                        if (b < NBP and KV == 7) or (b >= NBP and KV == 6):
                            continue
                        tb_ = B_TR if si == 0 else B_X
                        if b < NBP:
                            tr(ps[tb_][:, 0:128], xf[:, o0:o0 + 128], [xft], [ps_t[tb_]])
                            cp("dve", stg[si][:, 0:128], ps[tb_][:, 0:128], [ps_t[tb_]], [stg_t[si]])
                            if which == 1:
                                cp("pool", Vtok[:, b, :], stg[si][:, 0:128], [stg_t[si]], [V_t])
                            if b == 0 and KV == 3:
                                pass
                            elif b == 0:
                                (lambda *a, **k: None if (KV == 5 or os.environ.get("NODMA")) else dma(*a, **k))(dst_p[l, 0:16, hh * 128:(hh + 1) * 128], stg[si][112:128, 0:128], [stg_t[si]], [],
                                    is_out=True)
                            else:
                                r0 = 16 + (b - 1) * 128
                                (lambda *a, **k: None if (KV == 5 or os.environ.get("NODMA")) else dma(*a, **k))(dst_p[l, r0:r0 + 128, hh * 128:(hh + 1) * 128], stg[si][:, 0:128], [stg_t[si]], [],
                                    is_out=True)
                        else:
                            tr(ps[tb_][:, 0:128], xf[:, o0:o0 + 128], [xft], [ps_t[tb_]])
                            cp("dve", stg[si][:, 0:128], ps[tb_][:, 0:128], [ps_t[tb_]], [stg_t[si]])
                            r0 = (b - NBP) * 128
                            dma(dst_s[l, r0:r0 + 128, hh * 128:(hh + 1) * 128], stg[si][:, 0:128], [stg_t[si]], [],
                                is_out=True)
                if which == 1:
                    for s_ in range(4):
                        k0 = TP + s_ * 64
                        for k in range(8):
                            mm(ps[B_TR][0:64, 0:128], xnT[:, k, k0:k0 + 64], wslot[:, k, :], k == 0, k == 7,
                               [wt_] + xnr(k0, 64), [ps_t[B_TR]])
                        cp("act", Vs[:, s_, :], ps[B_TR][0:64, 0:128], [ps_t[B_TR]], [Vs_t])
            wq, wqt = W.get()
            wg, wgt = W.get()
            KB = int(os.environ.get('KB', '9'))

            def qproj(c0, n):
                proj(wq, wqt, c0, n, B_PA)
                cp("act", wk[0][:, 0:n], ps[B_PA][:, 0:n], [ps_t[B_PA]], [wk_t[0]])
                act(wk[1][:, 0:n], wk[0][:, 0:n], AF.Square, [wk_t[0]], [wk_t[1]])
                mm(ps[B_ST][:, 0:n], onesf[:], wk[1][:, 0:n], True, True, [wk_t[1], c_t], [ps_t[B_ST]])
                rstd_from(B_ST, n, 1.0 / 128.0, 4)
                stt("dve", qn[:, 0:n], wk[0][:, 0:n], gqs[:, l:l + 1], wk[4][:, 0:n], ALU.mult, ALU.mult,
                    [wk_t[0], wk_t[4], c_t], [qn_t])

            def gate_out(c0, n, avcols):
                proj(wg, wgt, c0, n, B_PB)
                act(wk[5][:, 0:n], ps[B_PB][:, 0:n], AF.Silu, [ps_t[B_PB]], [wk_t[5]])
                tt("dve", yT[:, hh, c0:c0 + n], ps[B_AV][:, avcols:avcols + n], wk[5][:, 0:n], ALU.mult,
                   [ps_t[B_AV], wk_t[5]], yw(hh, c0, n))

            for b0 in (range(0, NBP, 4) if KB >= 2 else []):
                b1 = min(b0 + 4, NBP)
                nq = (b1 - b0) * 128
                c0 = b0 * 128
                qproj(c0, nq)
                memset("pool", sacc[:, 0:nq], 0.0, [sacc_t])
                kbs = list(range(b1 - 1, -1, -1))
                state = {"first_av": True}

                def stage1(i, kb):
                    pi = i % 2
                    zb = B_Z0 + pi
                    qo = (max(kb, b0) - b0) * 128
                    mm(ps[zb][:, qo:nq], knT[:, kb * 128:(kb + 1) * 128], qn[:, qo:nq], True, True,
                       [knT_t, qn_t], [ps_t[zb]])
                    act(ew[pi][:, qo:nq], ps[zb][:, qo:nq], AF.Exp, [ps_t[zb]], [ew_t[pi]])
                    act(sp[pi][:, qo:nq], ew[pi][:, qo:nq], AF.Ln, [ew_t[pi]], [sp_t[pi]], bias=1.0)
                    if kb >= b0:
                        tt("pool", sp[pi][:, qo:qo + 128], sp[pi][:, qo:qo + 128], mask01[:], ALU.mult,
                           [sp_t[pi], c_t], [sp_t[pi]])

                def stage2(i, kb):
                    pi = i % 2
                    zb = B_Z0 + pi
                    qo = (max(kb, b0) - b0) * 128
                    first = (i == 0)
                    mm(ps[zb][:, qo:nq], knT[:, kb * 128:(kb + 1) * 128], qn[:, qo:nq], True, False,
                       [knT_t, qn_t, ew_t[pi]], [ps_t[zb]])
                    mm(ps[zb][:, qo:nq], negtri[:], sp[pi][:, qo:nq], False, first, [sp_t[pi], c_t, ps_t[zb]],
                       [ps_t[zb]])
                    if not first:
                        mm(ps[zb][:, qo:nq], negones[:], sacc[:, qo:nq], False, True, [sacc_t, c_t, ps_t[zb]],
                           [ps_t[zb]])
                    tt("pool", sacc[:, qo:nq], sacc[:, qo:nq], sp[pi][:, qo:nq], ALU.add, [sacc_t, sp_t[pi]], [sacc_t])
                    act(wT[pi][:, qo:nq], ps[zb][:, qo:nq], AF.Exp, [ps_t[zb]], [wT_t[pi]])
                    if kb >= b0:
                        tt("pool", wT[pi][:, qo:qo + 128], wT[pi][:, qo:qo + 128], mask01[:], ALU.mult,
                           [wT_t[pi], c_t], [wT_t[pi]])
                    mm(ps[B_AV][:, qo:nq], Vtok[:, kb, :], wT[pi][:, qo:nq], first, i == len(kbs) - 1,
                       [V_t, wT_t[pi], ps_t[B_AV]], [ps_t[B_AV]], sgc=True)

                stage1(0, kbs[0])
                for i, kb in enumerate(kbs):
                    if i + 1 < len(kbs):
                        stage1(i + 1, kbs[i + 1])
                    stage2(i, kb)
                gate_out(c0, nq, 0)

            qproj(TP, 256)
            qns = qn
            for s in (range(4) if KB >= 3 else []):
                qc = qns[:, s * 64:(s + 1) * 64]
                kc0 = TP + s * 64
                pieces = list(range(cfg.NPC - 1, -1, -1))

                def load_piece(i):
                    pc = pieces[i]
                    bi = i % 2
                    dma(kst[bi][:], ck[l, s, pc * 512:(pc + 1) * 512, hh, :].rearrange("(j p) d -> p j d", p=128),
                        [], [kst_t[bi]], q="act")
                    dma(vst[bi][:], cv[l, s, pc * 512:(pc + 1) * 512, hh, :].rearrange("(j p) d -> p j d", p=128),
                        [], [vst_t[bi]], q="act")
                load_piece(0)
                zb = B_Z0
                mm(ps[zb][0:64, 0:64], knT[:, kc0:kc0 + 64], qc, True, True, [knT_t, qn_t], [ps_t[zb]])
                act(ew[0][0:64, 0:64], ps[zb][0:64, 0:64], AF.Exp, [ps_t[zb]], [ew_t[0]])
                act(sp[0][0:64, 0:64], ew[0][0:64, 0:64], AF.Ln, [ew_t[0]], [sp_t[0]], bias=1.0)
                tt("pool", sp[0][0:64, 0:64], sp[0][0:64, 0:64], mask01[0:64, 0:64], ALU.mult, [sp_t[0], c_t], [sp_t[0]])
                mm(ps[zb][0:64, 0:64], knT[:, kc0:kc0 + 64], qc, True, False, [knT_t, qn_t, ew_t[0]], [ps_t[zb]])
                mm(ps[zb][0:64, 0:64], negtri[0:64, 0:64], sp[0][0:64, 0:64], False, True, [sp_t[0], c_t, ps_t[zb]],
                   [ps_t[zb]])
                memset("pool", sacc[:, 0:64], 0.0, [sacc_t])
                cp("pool", sacc[0:64, 0:64], sp[0][0:64, 0:64], [sp_t[0]], [sacc_t])
                act(wT[0][0:64, 0:64], ps[zb][0:64, 0:64], AF.Exp, [ps_t[zb]], [wT_t[0]])
                tt("pool", wT[0][0:64, 0:64], wT[0][0:64, 0:64], mask01[0:64, 0:64], ALU.mult, [wT_t[0], c_t], [wT_t[0]])
                mm(ps[B_AV][:, 0:64], Vs[:, s, :], wT[0][0:64, 0:64], True, False, [Vs_t, wT_t[0]], [ps_t[B_AV]])
                for i, pc in enumerate(pieces):
                    bi = i % 2
                    pi = (i + 1) % 2
                    zb = B_Z0 + pi
                    if i + 1 < len(pieces):
                        load_piece(i + 1)
                    tb_ = B_TR if bi == 0 else B_X
                    for j in range(4):
                        tr(ps[tb_][:, j * 128:(j + 1) * 128], kst[bi][:, j, :], [kst_t[bi]], [ps_t[tb_]])
                    cp("dve", kcT[bi][:], ps[tb_][:], [ps_t[tb_]], [kcT_t[bi]])
                    cp("pool", vc[bi][:], vst[bi][:], [vst_t[bi]], [vc_t[bi]])
                    for j in range(4):
                        mm(ps[zb][:, j * 64:(j + 1) * 64], kcT[bi][:, j * 128:(j + 1) * 128], qc, j == 0, j == 3,
                           [kcT_t[bi], qn_t], [ps_t[zb]])
                    act(ew[pi][:, 0:256], ps[zb][:, 0:256], AF.Exp, [ps_t[zb]], [ew_t[pi]])
                    act(sp[pi][:, 0:256], ew[pi][:, 0:256], AF.Ln, [ew_t[pi]], [sp_t[pi]], bias=1.0)
                    sp3 = sp[pi][:, 0:256].rearrange("p (j n) -> p j n", j=4)
                    cp("pool", cs[:, 3, :], sacc[:, 0:64], [sacc_t], [cs_t])
                    for j in (2, 1, 0):
                        tt("pool", cs[:, j, :], cs[:, j + 1, :], sp3[:, j + 1, :], ALU.add, [cs_t, sp_t[pi]], [cs_t])
                    tt("pool", sacc[:, 0:64], cs[:, 0, :], sp3[:, 0, :], ALU.add, [cs_t, sp_t[pi]], [sacc_t])
                    for j in range(4):
                        mm(ps[zb][:, j * 64:(j + 1) * 64], kcT[bi][:, j * 128:(j + 1) * 128], qc, j == 0, False,
                           [kcT_t[bi], qn_t, ew_t[pi]], [ps_t[zb]])
                    mm(ps[zb][:, 0:256], negtri[:], sp[pi][:, 0:256], False, False, [sp_t[pi], c_t, ps_t[zb]],
                       [ps_t[zb]])
                    mm(ps[zb][:, 0:256], negones[:], cs[:].rearrange("p j n -> p (j n)"), False, True,
                       [cs_t, c_t, ps_t[zb]], [ps_t[zb]])
                    act(wT[pi][:, 0:256], ps[zb][:, 0:256], AF.Exp, [ps_t[zb]], [wT_t[pi]])
                    for j in range(4):
                        mm(ps[B_AV][:, 0:64], vc[bi][:, j, :], wT[pi][:, j * 64:(j + 1) * 64], False,
                           (i == len(pieces) - 1) and j == 3, [vc_t[bi], wT_t[pi], ps_t[B_AV]], [ps_t[B_AV]])
                gate_out(kc0, 64, 0)
        return alltr

    def branch_c(l, ph):
        def a(name, shape, dt):
            return ph.enter_context(nc.sbuf_tensor(name + "_%d" % l, shape, dt))
        Gs = [a("pc_G%d" % i, [128, 640], F32) for i in range(2)]
        kcfs = [a("pc_kcf%d" % i, [128, 512], F32) for i in range(2)]
        ktbs = [a("pc_ktb%d" % i, [128, 512], BF16) for i in range(2)]
        qtbs = [a("pc_qtb%d" % i, [128, 512], BF16) for i in range(2)]
        ktoks = [a("pc_ktok%d" % i, [128, 4, 128], BF16) for i in range(2)]
        vtoks = [a("pc_vtok%d" % i, [128, 4, 128], BF16) for i in range(2)]
        facs = [a("pc_fac%d" % i, [128, 3, 8], F32) for i in range(2)]
        wkx = [a("pc_wk%d" % i, [128, 512], F32) for i in range(6)]
        wkc = [[wk[i] for i in range(6)], wkx]
        Sa = [a("pc_S%d" % i, [128, 128], F32) for i in range(2)]
        Sst = a("pc_Sst", [128, 128], F32)
        Stmp = a("pc_Stmp", [128, 128], F32)
        Sbf = [a("pc_Sbf%d" % i, [128, 128], BF16) for i in range(2)]
        attb = [a("pc_attb%d" % i, [128, 64], BF16) for i in range(2)]
        Sst_t, Stmp_t = PT(), PT()
        Gs_t, kcfs_t, ktbs_t, qtbs_t, ktoks_t, vtoks_t, facs_t = [[PT(), PT()] for _ in range(7)]
        wkx_t = [PT() for _ in range(6)]
        wkc_t = [[wk_t[i] for i in range(6)], wkx_t]
        tcnt = [0]
        Sa_t = [PT(), PT()]
        Sbf_t = [PT(), PT()]
        attb_t = [PT(), PT()]
        alltr = ([Sst_t, Stmp_t] + Sa_t + Sbf_t + attb_t + Gs_t + kcfs_t + ktbs_t + qtbs_t + ktoks_t + vtoks_t
                 + facs_t + wkx_t)
        nchunk_total = T // 64
        for hh in range(8):
            wf, wft = W.get()
            wq, wqt = W.get()
            wi, wit = W.get()
            wg, wgt = W.get()
            col = l * 8 + hh
            oml = lbv[:, 1, col:col + 1]
            lbf = lbv[:, 2, col:col + 1]
            noml = lbv[:, 3, col:col + 1]
            memset("dve", Gs[tcnt[0] % 2][:, 0:1], 0.0, [Gs_t[tcnt[0] % 2]])
            memset("dve", Sa[0][:], 0.0, [Sa_t[0]])
            cur = 0
            sidx = 0
            for ti, (c0, n) in enumerate(tiles):
                nch = n // 64
                par = tcnt[0] % 2
                tcnt[0] += 1
                G, G_t = Gs[par], Gs_t[par]
                Gp, Gp_t = Gs[1 - par], Gs_t[1 - par]
                kcf, kcf_t = kcfs[par], kcfs_t[par]
                ktb, ktb_t = ktbs[par], ktbs_t[par]
                qtb, qtb_t = qtbs[par], qtbs_t[par]
                ktok, ktok_t = ktoks[par], ktoks_t[par]
                vtok, vtok_t = vtoks[par], vtoks_t[par]
                fac, fac_t = facs[par], facs_t[par]
                w, w_t = wkc[par], wkc_t[par]
                proj(wf, wft, c0, n, B_PA)
                act(w[0][:, 0:n], ps[B_PA][:, 0:n], AF.Sigmoid, [ps_t[B_PA]], [w_t[0]])
                ts("dve", w[1][:, 0:n], w[0][:, 0:n], oml, lbf, ALU.mult, ALU.add, [w_t[0], c_t], [w_t[1]])
                act(w[1][:, 0:n], w[1][:, 0:n], AF.Ln, [w_t[1]], [w_t[1]])
                ts("dve", kcf[:, 0:n], w[0][:, 0:n], noml, oml, ALU.mult, ALU.add, [w_t[0], c_t], [kcf_t])
                if ti > 0:
                    pn = tiles[ti - 1][1]
                    cp("dve", G[:, 0:1], Gp[:, pn:pn + 1], [Gp_t], [G_t])
                S.op("dve", lambda h, n=n, G=G, w=w: h.tensor_tensor_scan(G[:, 1:n + 1], onesb[:, 0:n],
                                                                 w[1][:, 0:n], G[:, 0:1], ALU.mult, ALU.add),
                     [w_t[1], G_t, c_t], [G_t])
                G3 = G[:, 1:n + 1].rearrange("p (j t) -> p j t", t=64)
                gmid = G[:, 32:32 + n].rearrange("p (j t) -> p j t", t=64)[:, :, 0:1].to_broadcast([128, nch, 64])
                tt("dve", w[2][:, 0:n].rearrange("p (j t) -> p j t", t=64), G3, gmid, ALU.subtract, [G_t], [w_t[2]])
                act(w[3][:, 0:n], w[2][:, 0:n], AF.Exp, [w_t[2]], [w_t[3]])
                act(w[2][:, 0:n], w[2][:, 0:n], AF.Exp, [w_t[2]], [w_t[2]], scale=-1.0)
                gs = G[:, 0:n].rearrange("p (j t) -> p j t", t=64)[:, :, 0]
                gm = G[:, 32:32 + n].rearrange("p (j t) -> p j t", t=64)[:, :, 0]
                ge = G[:, 64:64 + n].rearrange("p (j t) -> p j t", t=64)[:, :, 0]
                tt("dve", fac[:, 0, 0:nch], ge, gs, ALU.subtract, [G_t], [fac_t])
                tt("dve", fac[:, 1, 0:nch], ge, gm, ALU.subtract, [G_t], [fac_t])
                tt("dve", fac[:, 2, 0:nch], gm, gs, ALU.subtract, [G_t], [fac_t])
                act(fac[:, :, 0:nch], fac[:, :, 0:nch], AF.Exp, [fac_t], [fac_t])
                proj(wq, wqt, c0, n, B_PB)
                act(w[0][:, 0:n], ps[B_PB][:, 0:n], AF.Silu, [ps_t[B_PB]], [w_t[0]])
                tt("dve", qtb[:, 0:n], w[0][:, 0:n], w[3][:, 0:n], ALU.mult, [w_t[0], w_t[3]], [qtb_t])
                tt("dve", kcf[:, 0:n], kcf[:, 0:n], w[2][:, 0:n], ALU.mult, [kcf_t, w_t[2]], [kcf_t])
                cp("pool", ktb[:, 0:n], kcf[:, 0:n], [kcf_t], [ktb_t])
                proj(wi, wit, c0, n, B_PA)
                cp("act", w[1][:, 0:n], ps[B_PA][:, 0:n], [ps_t[B_PA]], [w_t[1]])
                nb4 = n // 128
                for j in range(nb4):
                    tr(ps[B_TR][:, j * 128:(j + 1) * 128], kcf[:, j * 128:(j + 1) * 128], [kcf_t], [ps_t[B_TR]])
                cp("dve", ktok[:, 0:nb4, :], ps[B_TR][:, 0:n].rearrange("p (j n) -> p j n", n=128), [ps_t[B_TR]], [ktok_t])
                for j in range(nb4):
                    tr(ps[B_TR][:, j * 128:(j + 1) * 128], w[1][:, j * 128:(j + 1) * 128], [w_t[1]], [ps_t[B_TR]])
                cp("act", vtok[:, 0:nb4, :], ps[B_TR][:, 0:n].rearrange("p (j n) -> p j n", n=128), [ps_t[B_TR]], [vtok_t])
                for j in range(nch):
                    gch = c0 // 64 + j
                    jb, jr = j // 2, (j % 2) * 64
                    is_smp = gch >= TP // 64
                    if is_smp:
                        s = gch - TP // 64
                        dma(Sst[:], shg[l, s, hh], [], [Sst_t], q="act")
                        Sprev, Sprev_t = Sst, Sst_t
                        Snew, Snew_t = Stmp, Stmp_t
                    else:
                        Sprev, Sprev_t = Sa[cur], Sa_t[cur]
                        Snew, Snew_t = Sa[1 - cur], Sa_t[1 - cur]
                    sb_i = sidx % 2
                    sidx += 1
                    ts("pool", Sbf[sb_i][:], Sprev[:], fac[:, 2, j:j + 1], None, ALU.mult, None, [Sprev_t, fac_t],
                       [Sbf_t[sb_i]])
                    pbk = B_X if j % 2 == 0 else B_Z1
                    abk = B_Z0 if j % 2 == 0 else B_PB
                    mm(ps[pbk][:, 0:128], ktok[jr:jr + 64, jb, :], vtok[jr:jr + 64, jb, :], True, True,
                       [ktok_t, vtok_t], [ps_t[pbk]])
                    ts("dve", Snew[:], Sprev[:], fac[:, 0, j:j + 1], None, ALU.mult, None, [Sprev_t, fac_t, Snew_t], [Snew_t])
                    stt("dve", Snew[:], ps[pbk][:, 0:128], fac[:, 1, j:j + 1], Snew[:], ALU.mult, ALU.add,
                        [ps_t[pbk], fac_t, Snew_t], [Snew_t])
                    mm(ps[abk][jr:jr + 64, 0:64], ktb[:, j * 64:(j + 1) * 64], qtb[:, j * 64:(j + 1) * 64], True, True,
                       [ktb_t, qtb_t], [ps_t[abk]])
                    tt("dve", attb[sb_i][jr:jr + 64, :], ps[abk][jr:jr + 64, 0:64], maskle2[jr:jr + 64, jr // 64, :], ALU.mult,
                       [ps_t[abk], c_t], [attb_t[sb_i]])
                    mm(ps[B_AV][:, j * 64:(j + 1) * 64], vtok[jr:jr + 64, jb, :], attb[sb_i][jr:jr + 64, :], j == 0, False,
                       [vtok_t, attb_t[sb_i], ps_t[B_AV]], [ps_t[B_AV]])
                    mm(ps[B_AV][:, j * 64:(j + 1) * 64], Sbf[sb_i][:], qtb[:, j * 64:(j + 1) * 64], False, j == nch - 1,
                       [Sbf_t[sb_i], qtb_t, ps_t[B_AV]], [ps_t[B_AV]])
                    if is_smp:
                        dma(hs[l, s, hh], Snew[:], [Snew_t], [], is_out=True)
                    else:
                        cur = 1 - cur
                        if gch == TP // 64 - 1:
                            dma(hp[l, hh], Snew[:], [Snew_t], [], is_out=True)
                cp("act", w[0][:, 0:n], ps[B_AV][:, 0:n], [ps_t[B_AV]], [w_t[0]])
                act(w[1][:, 0:n], w[0][:, 0:n], AF.Square, [w_t[0]], [w_t[1]])
                mm(ps[B_ST][:, 0:n], onesf[:], w[1][:, 0:n], True, True, [w_t[1], c_t], [ps_t[B_ST]])
                act(w[4][:, 0:n], ps[B_ST][:, 0:n], AF.Ln, [ps_t[B_ST]], [w_t[4]], bias=EPS, scale=1.0 / 128.0)
                act(w[4][:, 0:n], w[4][:, 0:n], AF.Exp, [w_t[4]], [w_t[4]], scale=-0.5)
                stt("dve", w[0][:, 0:n], w[0][:, 0:n], vec[:, V_GH + l:V_GH + l + 1], w[4][:, 0:n],
                    ALU.mult, ALU.mult, [w_t[0], w_t[4], c_t], [w_t[0]])
                proj(wg, wgt, c0, n, B_PB)
                act(w[5][:, 0:n], ps[B_PB][:, 0:n], AF.Silu, [ps_t[B_PB]], [w_t[5]])
                tt("dve", yT[:, hh, c0:c0 + n], w[0][:, 0:n], w[5][:, 0:n], ALU.mult, [w_t[0], w_t[5]],
                   yw(hh, c0, n))
        return alltr

    def run_phase(fn, l):
        ph = contextlib.ExitStack()
        trks = fn(l, ph)
        last_bar[0] = memset("pool", bar[:], 0.0, trks)
        ph.close()

    for l in range(D):
        plan_layer(l)
    import os
    KSTOP = int(os.environ.get("KSTOP", "99"))
    for l in range(D):
        if KSTOP < 99 and l > 0:
            break
        if KSTOP >= 1:
            norm_phase(l)
        if KSTOP >= 2:
            run_phase(branch_a, l)
        if KSTOP >= 3:
            merge_phase(l, 0)
        if KSTOP >= 4:
            run_phase(branch_b, l)
        if KSTOP >= 5:
            merge_phase(l, 1)
        if KSTOP >= 6:
            run_phase(branch_c, l)
        if KSTOP >= 7:
            merge_phase(l, 2)

    for b in range(1, NB):
        si = b % 2
        ti = (b * 128) // 512
        for half in range(2):
            hb = wk[2 + half]
            hbt = wk_t[2 + half]
            dma(hb[:].rearrange("p (j n) -> p j n", j=4), hd[:, half * 4:half * 4 + 4, b * 128:(b + 1) * 128],
                [hd_t[c][ti] for c in range(half * 4, half * 4 + 4)], [hbt])
            bank = B_PA + half
            for j in range(4):
                tr(ps[bank][:, j * 128:(j + 1) * 128], hb[:, j * 128:(j + 1) * 128], [hbt], [ps_t[bank]])
            cp("act" if half else "dve", stg[si][:, half * 512:(half + 1) * 512], ps[bank][:], [ps_t[bank]], [stg_t[si]])
        if b < NBP:
            dma(yp[(b - 1) * 128:b * 128, :], stg[si][:], [stg_t[si]], [], is_out=True)
        else:
            dma(ys[(b - NBP) * 128:(b - NBP + 1) * 128, :], stg[si][:], [stg_t[si]], [], is_out=True)

    S.emit(st)
    st.close()
    return nc


def host_prep(cfg, inputs):
    D = cfg.DEPTH
    f = lambda a: np.ascontiguousarray(np.asarray(a, dtype=np.float32))

    def pc(v):
        return np.asarray(v, np.float32).reshape(D, 8, 128).transpose(2, 0, 1).reshape(128, D * 8)
    vecs = np.zeros((128, NVEC), np.float32)
    vecs[:, 0:8 * D] = pc(inputs["norm_g"])
    vecs[:, 8 * D:16 * D] = pc(inputs["pool_scale"])
    vecs[:, 16 * D:24 * D] = pc(inputs["hgrn_lower_bounds"])
    vecs[:, 24 * D:25 * D] = np.asarray(inputs["q_norm_g"], np.float32).T
    vecs[:, 25 * D:26 * D] = np.asarray(inputs["k_norm_g"], np.float32).T
    vecs[:, 26 * D:27 * D] = np.asarray(inputs["hgrn_norm_g"], np.float32).T
    consts = np.zeros((128, 64), np.float32)
    for g, w in enumerate((2, 4, 8, 16)):
        consts[:, g * 16:(g + 1) * 16] = 1.0 / np.minimum(np.arange(16) + 1.0, float(w))
    shared = dict(meta=f(inputs["meta_tokens"]), vecs=vecs, consts=consts, w_in=f(inputs["w_in"]),
                  w_pool=f(inputs["w_pool"]), w_branch=f(inputs["w_branch"]), w_out=f(inputs["w_out"]))
    maps = []
    for c in range(8):
        m = dict(shared)
        m["xp"] = f(inputs["x_prompt"][c])
        m["xs"] = f(np.asarray(inputs["x_sample"][4 * c:4 * c + 4]).reshape(256, 1024))
        m["ck"] = f(np.asarray(inputs["cache_k"])[:, 4 * c:4 * c + 4])
        m["cv"] = f(np.asarray(inputs["cache_v"])[:, 4 * c:4 * c + 4])
        m["spool"] = f(np.asarray(inputs["state_pool"])[:, 4 * c:4 * c + 4])
        m["shg"] = f(np.asarray(inputs["state_hgrn"])[:, 4 * c:4 * c + 4])
        maps.append(m)
    return maps


def gather(cfg, res):
    D = cfg.DEPTH
    r = res
    cat = lambda k, ax: np.concatenate([x[k] for x in r], axis=ax)
    y_prompt = np.stack([x["yp"] for x in r], 0)
    y_sample = np.concatenate([x["ys"].reshape(4, 64, 1024) for x in r], 0)
    k_prompt = np.stack([x["kp"].reshape(D, cfg.LP, 8, 128) for x in r], 1)
    v_prompt = np.stack([x["vp"].reshape(D, cfg.LP, 8, 128) for x in r], 1)
    pool_prompt = np.stack([x["pp"] for x in r], 1)
    hgrn_prompt = np.stack([x["hp"] for x in r], 1)
    k_sample = np.concatenate([x["ks"].reshape(D, 4, 64, 8, 128) for x in r], 1)
    v_sample = np.concatenate([x["vs"].reshape(D, 4, 64, 8, 128) for x in r], 1)
    pool_sample = np.concatenate([x["pls"] for x in r], 1)
    hgrn_sample = np.concatenate([x["hs"] for x in r], 1)
    outs = (y_prompt, y_sample, k_prompt, v_prompt, pool_prompt, hgrn_prompt, k_sample, v_sample, pool_sample,
            hgrn_sample)
    return tuple(np.ascontiguousarray(o, dtype=np.float32) for o in outs)


def kernel(**inputs):
    seq = int(np.asarray(inputs["x_prompt"]).shape[1])
    past = int(np.asarray(inputs["cache_k"]).shape[2])
    depth = int(np.asarray(inputs["w_in"]).shape[0])
    cfg = Cfg(seq, past, depth)
    nc = build(cfg)
    maps = host_prep(cfg, inputs)
    res = run_bass_kernel_spmd(nc, maps, core_ids=list(range(8)))
    return gather(cfg, res.results)
```

```python
import contextlib
import os
import numpy as np
import concourse.bass as bass
import concourse.mybir as mybir
from concourse.bass_utils import run_bass_kernel_spmd

F32 = mybir.dt.float32
BF16 = mybir.dt.bfloat16
ALU = mybir.AluOpType
AF = mybir.ActivationFunctionType

NDMA_SEMS = 8
P = 128
EPS = 1e-6


class Trk:
    __slots__ = ("w", "r")

    def __init__(self):
        self.w = None
        self.r = []


class Op:
    __slots__ = ("eng", "fn", "deps", "is_dma", "needs_inc", "token", "dma_slot")

    def __init__(self, eng, fn, deps, is_dma):
        self.eng = eng
        self.fn = fn
        self.deps = deps
        self.is_dma = is_dma
        self.needs_inc = False
        self.token = None
        self.dma_slot = None


class Sched:
    ENGS = ("pe", "act", "dve", "pool", "sp")

    def __init__(self, nc):
        self.nc = nc
        self.q = {e: [] for e in self.ENGS}
        self.ndma = {e: 0 for e in self.ENGS}
        self.out_dmas = []

    def op(self, eng, fn, reads=(), writes=(), is_dma=False, is_out=False):
        deps = []
        seen = set()

        def add(o):
            if o is not None and id(o) not in seen:
                seen.add(id(o))
                deps.append(o)
        for t in reads:
            add(t.w)
        for t in writes:
            add(t.w)
            for r in t.r:
                add(r)
        o = Op(eng, fn, deps, is_dma)
        if is_dma:
            o.dma_slot = self.ndma[eng]
            self.ndma[eng] += 1
        for t in reads:
            if not is_dma:
                t.r = [r for r in t.r if r.is_dma or r.eng != eng]
            t.r.append(o)
        for t in writes:
            t.w = o
            t.r = []
        self.q[eng].append(o)
        if is_out:
            self.out_dmas.append(o)
        return o

    def emit(self, st):
        nc = self.nc
        esem = {e: st.enter_context(nc.semaphore("s_" + e)) for e in self.ENGS}
        dsem = {e: [st.enter_context(nc.semaphore("d_%s%d" % (e, i))) for i in range(NDMA_SEMS)]
                for e in self.ENGS if self.ndma[e] > 0}
        for e in self.ENGS:
            for o in self.q[e]:
                for d in o.deps:
                    if d.is_dma:
                        continue
                    if d.eng == "pe" and o.eng == "pe" and not o.is_dma:
                        continue
                    d.needs_inc = True
        for e in self.ENGS:
            c = 0
            for o in self.q[e]:
                if o.is_dma:
                    s = dsem[e][o.dma_slot % NDMA_SEMS]
                    o.token = (s, 16 * (o.dma_slot // NDMA_SEMS + 1))
                elif o.needs_inc:
                    c += 1
                    o.token = (esem[e], c)
        final = {}
        for e in self.ENGS:
            for o in self.q[e]:
                if o.is_dma:
                    final[id(o.token[0])] = o.token
        block = st.enter_context(nc.Block())

        def run_queue(e, h):
            waited = {}

            def wait(tok):
                s, v = tok
                k = id(s)
                if waited.get(k, 0) < v:
                    h.wait_ge(s, v)
                    waited[k] = v
            for o in self.q[e]:
                for d in o.deps:
                    if (not d.is_dma) and d.eng == "pe" and e == "pe" and not o.is_dma:
                        continue
                    wait(d.token)
                if o.is_dma and o.dma_slot >= NDMA_SEMS:
                    s, v = o.token
                    wait((s, v - 16))
                ins = o.fn(h)
                if o.is_dma:
                    ins.then_inc(o.token[0], 16)
                elif o.needs_inc:
                    ins.then_inc(o.token[0], 1)
            if e == "sp":
                for tok in final.values():
                    wait(tok)

        @block.tensor
        def _(h):
            run_queue("pe", h)

        @block.scalar
        def _(h):
            run_queue("act", h)

        @block.vector
        def _(h):
            run_queue("dve", h)

        @block.gpsimd
        def _(h):
            run_queue("pool", h)

        @block.sync
        def _(h):
            run_queue("sp", h)


class Cfg:
    def __init__(self, seq=2048, past=2048, depth=2):
        self.SEQ = seq
        self.PAST = past
        self.DEPTH = depth
        self.NBP = 1 + seq // 128
        self.NB = self.NBP + 2
        self.T = self.NB * 128
        self.TP = self.NBP * 128
        self.LP = 16 + seq
        self.tiles = [(s, min(512, self.T - s)) for s in range(0, self.T, 512)]
        self.NPC = past // 512


NVEC = 64


def build(cfg):
    nc = bass.Bass("TRN2", target_bir_lowering=False)
    D = cfg.DEPTH
    T, TP, NB, NBP = cfg.T, cfg.TP, cfg.NB, cfg.NBP
    tiles = cfg.tiles
    NT = len(tiles)

    def din(name, shape):
        return nc.dram_tensor(name, shape, F32, kind="ExternalInput").ap()

    def dout(name, shape):
        return nc.dram_tensor(name, shape, F32, kind="ExternalOutput").ap()

    xp = din("xp", [cfg.SEQ, 1024])
    xs = din("xs", [256, 1024])
    ck = din("ck", [D, 4, cfg.PAST, 8, 128])
    cv = din("cv", [D, 4, cfg.PAST, 8, 128])
    spool = din("spool", [D, 4, 15, 1024])
    shg = din("shg", [D, 4, 8, 128, 128])
    meta = din("meta", [16, 1024])
    vecs = din("vecs", [128, NVEC])
    consts = din("consts", [128, 64])
    w_in = din("w_in", [D, 1024, 13 * 1024])
    w_pool = din("w_pool", [D, 4, 256, 256])
    w_branch = din("w_branch", [D, 3, 1024, 1024])
    w_out = din("w_out", [D, 1024, 1024])

    yp = dout("yp", [cfg.SEQ, 1024])
    ys = dout("ys", [256, 1024])
    kp = dout("kp", [D, cfg.LP, 1024])
    vp = dout("vp", [D, cfg.LP, 1024])
    pp = dout("pp", [D, 15, 1024])
    hp = dout("hp", [D, 8, 128, 128])
    ks = dout("ks", [D, 256, 1024])
    vs = dout("vs", [D, 256, 1024])
    pls = dout("pls", [D, 4, 15, 1024])
    hs = dout("hs", [D, 4, 8, 128, 128])
    hd = nc.dram_tensor("hscr", [128, 8, T], F32).ap()

    S = Sched(nc)
    st = contextlib.ExitStack()

    def sb(name, shape, dt):
        return st.enter_context(nc.sbuf_tensor(name, shape, dt))

    def psb(name):
        return st.enter_context(nc.psum_tensor(name, [128, 512], F32))

    xnT = st.enter_context(nc.sbuf_tensor("xnT", [128, 8, T], BF16, side="right"))
    yT = st.enter_context(nc.sbuf_tensor("yT", [128, 8, T], BF16, side="right"))
    tmpT = st.enter_context(nc.sbuf_tensor("tmpT", [128, 8, T], BF16, side="right"))
    xn_t = [Trk() for _ in range(NB)]
    y_t = [[Trk() for _ in range(NB)] for _ in range(8)]
    tmp_t = [[Trk() for _ in range(NT)] for _ in range(8)]
    hd_t = [[Trk() for _ in range(NT)] for _ in range(8)]
    NWS = 8
    wbf = [sb("wbf%d" % i, [128, 8, 128], BF16) for i in range(NWS)]
    wbf_t = [Trk() for _ in range(NWS)]
    wst = [sb("wst%d" % i, [128, 8, 128], F32) for i in range(2)]
    wst_t = [Trk() for _ in range(2)]
    stg = [sb("stg%d" % i, [128, 1024], F32) for i in range(2)]
    stg_t = [Trk() for _ in range(2)]
    NWK = 8
    wk = [sb("wk%d" % i, [128, 512], F32) for i in range(NWK)]
    wk_t = [Trk() for _ in range(NWK)]
    ident = sb("ident", [128, 128], F32)
    onesf = sb("onesf", [128, 128], F32)
    negtri = sb("negtri", [128, 128], BF16)
    negones = sb("negones", [128, 128], BF16)
    mask01 = sb("mask01", [128, 128], BF16)
    maskle2 = sb("maskle2", [128, 2, 64], F32)
    onesb = sb("onesb", [128, 512], BF16)
    vec = sb("vec", [128, NVEC], F32)
    cst = sb("cst", [128, 64], F32)
    lbv = sb("lbv", [128, 4, D * 8], F32)
    gqs = sb("gqs", [128, D], F32)
    c_t = Trk()
    ps = [psb("ps%d" % i) for i in range(8)]
    ps_t = [Trk() for _ in range(8)]
    B_PA, B_PB, B_ST, B_TR, B_Z0, B_Z1, B_AV, B_X = range(8)

    def blk(c0, n):
        return range(c0 // 128, (c0 + n + 127) // 128)

    def xnr(c0, n):
        return [xn_t[b] for b in blk(c0, n)]

    def yw(c, c0, n):
        return [y_t[c][b] for b in blk(c0, n)]

    def yr(c0, n):
        return [y_t[c][b] for c in range(8) for b in blk(c0, n)]

    def dma(out, in_, reads=(), writes=(), is_out=False, q="sp"):
        return S.op(q, lambda h: h.dma_start(out=out, in_=in_), reads, writes, is_dma=True, is_out=is_out)

    def mm(out, lhsT, rhs, start, stop, reads, writes, sgc=False):
        return S.op("pe", lambda h: h.matmul(out, lhsT=lhsT, rhs=rhs, start=start, stop=stop,
                                             skip_group_check=sgc), reads, writes)

    def tr(out, in_, reads, writes):
        k = in_.shape[0]
        return S.op("pe", lambda h: h.transpose(out, in_, ident[0:k, 0:k]), list(reads) + [c_t], writes)

    def act(out, in_, func, reads, writes, bias=None, scale=None):
        kw = {}
        if bias is not None:
            kw["bias"] = bias
        if scale is not None:
            kw["scale"] = scale
        return S.op("act", lambda h: h.activation(out, in_, func, **kw), reads, writes)

    def tt(eng, out, a, b, op, reads, writes):
        return S.op(eng, lambda h: h.tensor_tensor(out, a, b, op), reads, writes)

    def ts(eng, out, a, s1, s2, op0, op1, reads, writes):
        if op1 is None:
            return S.op(eng, lambda h: h.tensor_scalar(out, a, s1, None, op0), reads, writes)
        return S.op(eng, lambda h: h.tensor_scalar(out, a, s1, s2, op0, op1), reads, writes)

    def stt(eng, out, a, s, b, op0, op1, reads, writes):
        return S.op(eng, lambda h: h.scalar_tensor_tensor(out, a, s, b, op0, op1), reads, writes)

    def cp(eng, out, in_, reads, writes):
        if eng == "act":
            return S.op("act", lambda h: h.copy(out, in_), reads, writes)
        return S.op(eng, lambda h: h.tensor_copy(out, in_), reads, writes)

    def memset(eng, ap, v, writes):
        return S.op(eng, lambda h: h.memset(ap, v), (), writes)

    class WStream:
        def __init__(self):
            self.plan = []
            self.issued = 0
            self.taken = 0
            self.LOOK = 3

        def add(self, src, nk=8):
            self.plan.append((src, nk))

        def _issue(self):
            i = self.issued
            src, nk = self.plan[i]
            s_i = i % 2
            b_i = i % NWS
            dma(wst[s_i][:, 0:nk, :], src, writes=[wst_t[s_i]])
            cp("pool", wbf[b_i][:, 0:nk, :], wst[s_i][:, 0:nk, :], [wst_t[s_i]], [wbf_t[b_i]])
            self.issued += 1

        def get(self):
            i = self.taken
            while self.issued < len(self.plan) and self.issued <= i + self.LOOK:
                self._issue()
            self.taken += 1
            return wbf[i % NWS], wbf_t[i % NWS]

    W = WStream()

    def wsrc_in(l, grp, c):
        col = grp * 1024 + c * 128
        return w_in[l].rearrange("(k p) n -> p k n", p=128)[:, :, col:col + 128]

    def wsrc_br(l, n, c):
        return w_branch[l, n].rearrange("(k p) n -> p k n", p=128)[:, :, c * 128:(c + 1) * 128]

    def wsrc_out(l, c):
        return w_out[l].rearrange("(k p) n -> p k n", p=128)[:, :, c * 128:(c + 1) * 128]

    def wsrc_pool(l, g, e):
        return w_pool[l, g].rearrange("(k p) n -> p k n", p=128)[:, :, e * 128:(e + 1) * 128]

    G_UA, G_GA, G_QB, G_KB, G_VB, G_GB, G_FC, G_QC, G_IC, G_GC, G_MA, G_MB, G_MC = range(13)

    def plan_layer(l):
        for g in range(4):
            W.add(wsrc_in(l, G_UA, 2 * g))
            W.add(wsrc_in(l, G_UA, 2 * g + 1))
            for e in range(2):
                W.add(wsrc_pool(l, g, e), 2)
                W.add(wsrc_in(l, G_GA, 2 * g + e))
        plan_merge(l, 0, G_MA)
        for hh in range(8):
            W.add(wsrc_in(l, G_KB, hh))
            W.add(wsrc_in(l, G_VB, hh))
            W.add(wsrc_in(l, G_QB, hh))
            W.add(wsrc_in(l, G_GB, hh))
        plan_merge(l, 1, G_MB)
        for hh in range(8):
            W.add(wsrc_in(l, G_FC, hh))
            W.add(wsrc_in(l, G_QC, hh))
            W.add(wsrc_in(l, G_IC, hh))
            W.add(wsrc_in(l, G_GC, hh))
        plan_merge(l, 2, G_MC)

    def plan_merge(l, n, gm):
        for c in range(8):
            W.add(wsrc_br(l, n, c))
            W.add(wsrc_in(l, gm, c))
        for c in range(8):
            W.add(wsrc_out(l, c))

    def proj(wslot, wt, c0, n, bank, nk=8, src=None, src_reads=None):
        srcT = xnT if src is None else src
        rd = xnr(c0, n) if src_reads is None else src_reads
        for k in range(nk):
            mm(ps[bank][:, 0:n], wslot[:, k, :], srcT[:, k, c0:c0 + n], k == 0, k == nk - 1,
               [wt] + list(rd), [ps_t[bank]])

    def rstd_from(bank, n, scale, out_wk):
        act(wk[out_wk][:, 0:n], ps[bank][:, 0:n], AF.Ln, [ps_t[bank]], [wk_t[out_wk]], bias=EPS, scale=scale)
        act(wk[out_wk][:, 0:n], wk[out_wk][:, 0:n], AF.Exp, [wk_t[out_wk]], [wk_t[out_wk]], scale=-0.5)

    dma(vec[:], vecs[:, :], writes=[c_t])
    dma(cst[:], consts[:, :], writes=[c_t])
    memset("pool", onesf[:], 1.0, [c_t])
    memset("pool", onesb[:], 1.0, [c_t])
    memset("pool", negones[:], -1.0, [c_t])
    S.op("pool", lambda h: h.affine_select(ident[:], onesf[:], pattern=[[-1, 128]], compare_op=ALU.is_equal,
                                           fill=0.0, base=0, channel_multiplier=1), [c_t], [c_t])
    S.op("pool", lambda h: h.affine_select(negtri[:], negones[:], pattern=[[-1, 128]], compare_op=ALU.is_ge,
                                           fill=0.0, base=0, channel_multiplier=1), [c_t], [c_t])
    S.op("pool", lambda h: h.affine_select(mask01[:], onesb[:, 0:128], pattern=[[1, 128]], compare_op=ALU.is_gt,
                                           fill=0.0, base=0, channel_multiplier=-1), [c_t], [c_t])
    S.op("pool", lambda h: h.affine_select(maskle2[:], onesf[:].rearrange("p (a b) -> p a b", a=2),
                                           pattern=[[64, 2], [1, 64]], compare_op=ALU.is_ge, fill=0.0, base=0,
                                           channel_multiplier=-1), [c_t], [c_t])
    V_NG, V_PS, V_LB, V_GQ = 0, 8 * D, 16 * D, 24 * D
    V_GK, V_GH = V_GQ + D, V_GQ + 2 * D
    lbraw = vec[:, V_LB:V_LB + 8 * D].rearrange("p (l c) -> p l c", l=D)
    mx = wk[0][:, 0:8]
    ex = wk[0][:, 8:8 + 8 * D].rearrange("p (l c) -> p l c", l=D)
    sm = wk[0][:, 200:208]
    cp("dve", mx, lbraw[:, 0, :], [c_t], [wk_t[0]])
    for l in range(1, D):
        tt("dve", mx, mx, lbraw[:, l, :], ALU.max, [c_t, wk_t[0]], [wk_t[0]])
    for l in range(D):
        tt("dve", ex[:, l, :], lbraw[:, l, :], mx, ALU.subtract, [c_t, wk_t[0]], [wk_t[0]])
    act(wk[0][:, 8:8 + 8 * D], wk[0][:, 8:8 + 8 * D], AF.Exp, [wk_t[0]], [wk_t[0]])
    cp("dve", sm, ex[:, 0, :], [wk_t[0]], [wk_t[0]])
    for l in range(1, D):
        tt("dve", sm, sm, ex[:, l, :], ALU.add, [wk_t[0]], [wk_t[0]])
    S.op("dve", lambda h: h.reciprocal(sm, sm), [wk_t[0]], [wk_t[0]])
    for l in range(D):
        tt("dve", ex[:, l, :], ex[:, l, :], sm, ALU.mult, [wk_t[0]], [wk_t[0]])
    lb4 = lbv[:].rearrange("p f (l c) -> p f l c", l=D)
    memset("dve", lbv[:, 0, 0:8], 0.0, [c_t])
    for l in range(1, D):
        tt("dve", lb4[:, 0, l, :], lb4[:, 0, l - 1, :], ex[:, l, :], ALU.add, [wk_t[0], c_t], [c_t])
    ts("dve", lbv[:, 1, :], lbv[:, 0, :], -1.0, 1.0, ALU.mult, ALU.add, [c_t], [c_t])
    ts("dve", lbv[:, 2, :], lbv[:, 0, :], 1e-30, None, ALU.max, None, [c_t], [c_t])
    ts("dve", lbv[:, 3, :], lbv[:, 1, :], -1.0, None, ALU.mult, None, [c_t], [c_t])
    ts("dve", gqs[:], vec[:, V_GQ:V_GQ + D], float(128 ** -0.5), None, ALU.mult, None, [c_t], [c_t])

    def store_h_block(b, si):
        for half in range(2):
            bank = B_PA + half
            for j in range(4):
                c = half * 4 + j
                tr(ps[bank][:, j * 128:(j + 1) * 128], stg[si][:, c * 128:(c + 1) * 128], [stg_t[si]], [ps_t[bank]])
            o = wk[half]
            cp("act" if half else "dve", o[:], ps[bank][:], [ps_t[bank]], [wk_t[half]])
            ti = (b * 128) // 512
            dma(hd[:, half * 4:half * 4 + 4, b * 128:(b + 1) * 128],
                o[:].rearrange("p (j n) -> p j n", j=4), [wk_t[half]], [hd_t[c][ti] for c in range(half * 4, half * 4 + 4)])

    for b in range(NB):
        si = b % 2
        if b == 0:
            memset("pool", stg[si][:], 0.0, [stg_t[si]])
            dma(stg[si][112:128, :], meta[:, :], writes=[stg_t[si]])
        elif b < NBP:
            dma(stg[si][:], xp[(b - 1) * 128:b * 128, :], writes=[stg_t[si]])
        else:
            dma(stg[si][:], xs[(b - NBP) * 128:(b - NBP + 1) * 128, :], writes=[stg_t[si]])
        store_h_block(b, si)

    def norm_phase(l):
        for ti, (c0, n) in enumerate(tiles):
            for c in range(8):
                hb = wk[2 + (c % 2)]
                hbt = wk_t[2 + (c % 2)]
                dma(hb[:, 0:n], hd[:, c, c0:c0 + n], [hd_t[c][ti]], [hbt])
                act(hb[:, 0:n], hb[:, 0:n], AF.Square, [hbt], [hbt])
                mm(ps[B_ST][:, 0:n], onesf[:], hb[:, 0:n], c == 0, c == 7, [hbt, c_t], [ps_t[B_ST]])
            rstd_from(B_ST, n, 1.0 / 1024.0, 4)
            for c in range(8):
                hb = wk[5 + (c % 2)]
                hbt = wk_t[5 + (c % 2)]
                dma(hb[:, 0:n], hd[:, c, c0:c0 + n], [hd_t[c][ti]], [hbt])
                stt("dve", xnT[:, c, c0:c0 + n], hb[:, 0:n], vec[:, V_NG + l * 8 + c:V_NG + l * 8 + c + 1],
                    wk[4][:, 0:n], ALU.mult, ALU.mult, [hbt, wk_t[4], c_t], xnr(c0, n))

    def merge_phase(l, nbr):
        pairs = [(B_PA, B_PB), (B_Z0, B_Z1), (B_AV, B_X), (B_ST, B_TR)]
        wks = [0, 1, 6, 7]
        it = 0
        for c in range(8):
            wb, wbt = W.get()
            wm, wmt = W.get()
            for ti, (c0, n) in enumerate(tiles):
                pa, pb = pairs[it % 4]
                wi_ = wks[it % 4]
                it += 1
                proj(wb, wbt, c0, n, pa, src=yT, src_reads=yr(c0, n))
                proj(wm, wmt, c0, n, pb)
                act(wk[wi_][:, 0:n], ps[pb][:, 0:n], AF.Sigmoid, [ps_t[pb]], [wk_t[wi_]])
                tt("dve", tmpT[:, c, c0:c0 + n], ps[pa][:, 0:n], wk[wi_][:, 0:n], ALU.mult,
                   [ps_t[pa], wk_t[wi_]], [tmp_t[c][ti]])
        banks = [B_PA, B_PB, B_Z0, B_Z1]
        it = 0
        for c in range(8):
            wo, wot = W.get()
            for ti, (c0, n) in enumerate(tiles):
                bank = banks[it % 4]
                hb = wk[2 + (it % 4)]
                hbt = wk_t[2 + (it % 4)]
                it += 1
                dma(hb[:, 0:n], hd[:, c, c0:c0 + n], [hd_t[c][ti]], [hbt])
                for k in range(8):
                    mm(ps[bank][:, 0:n], wo[:, k, :], tmpT[:, k, c0:c0 + n], k == 0, k == 7,
                       [wot, tmp_t[k][ti]], [ps_t[bank]])
                tt("dve", hb[:, 0:n], hb[:, 0:n], ps[bank][:, 0:n], ALU.add, [hbt, ps_t[bank]], [hbt])
                dma(hd[:, c, c0:c0 + n], hb[:, 0:n], [hbt], [hd_t[c][ti]])

    bar = sb("bar", [128, 8], F32)
    last_bar = [None]

    def PT():
        t = Trk()
        t.w = last_bar[0]
        return t

    def branch_a(l, ph):
        U = ph.enter_context(nc.sbuf_tensor("pa_U_%d" % l, [128, T], F32))
        X = ph.enter_context(nc.sbuf_tensor("pa_X_%d" % l, [128, T], F32))
        Y = ph.enter_context(nc.sbuf_tensor("pa_Y_%d" % l, [128, T], F32))
        Dd = tmpT
        us = ph.enter_context(nc.sbuf_tensor("pa_us_%d" % l, [128, 4, 80], F32))
        ux = ph.enter_context(nc.sbuf_tensor("pa_ux_%d" % l, [128, 4, 80], F32))
        uy = ph.enter_context(nc.sbuf_tensor("pa_uy_%d" % l, [128, 4, 80], F32))
        ppre = ph.enter_context(nc.sbuf_tensor("pa_pre_%d" % l, [128, 4, 8, 16], F32))
        pst = ph.enter_context(nc.sbuf_tensor("pa_pst_%d" % l, [16, 5, 128], F32))
        U_t, X_t, Y_t, us_t, ux_t, uy_t, pre_t, pst_t = [PT() for _ in range(8)]
        D_t = [list(tmp_t[0]), list(tmp_t[1])]
        for s in range(4):
            si = s % 2
            dma(stg[si][0:15, :], spool[l, s], writes=[stg_t[si]])
            for c in range(8):
                tr(ps[B_TR][:, c * 16:c * 16 + 15], stg[si][0:15, c * 128:(c + 1) * 128], [stg_t[si]], [ps_t[B_TR]])
            cp("dve", ppre[:, s, :, 0:15], ps[B_TR][:, 0:128].rearrange("p (c r) -> p c r", r=16)[:, :, 0:15],
               [ps_t[B_TR]], [pre_t])
        for g in range(4):
            w = 2 << g
            nlev = g + 1
            for j in range(2):
                c = 2 * g + j
                wu, wut = W.get()
                for ti, (c0, n) in enumerate(tiles):
                    bank = B_PA + (ti % 2)
                    proj(wu, wut, c0, n, bank)
                    cp("act", U[:, c0:c0 + n], ps[bank][:, 0:n], [ps_t[bank]], [U_t])
                segs = [TP - 15] + [TP + 64 * s + 49 for s in range(4)]
                for i, s0 in enumerate(segs[0:4]):
                    tr(ps[B_TR][0:15, i * 128:(i + 1) * 128], U[:, s0:s0 + 15], [U_t], [ps_t[B_TR]])
                cp("dve", pst[0:15, 0:4, :], ps[B_TR][0:15, 0:512].rearrange("p (i n) -> p i n", i=4),
                   [ps_t[B_TR]], [pst_t])
                tr(ps[B_TR][0:15, 0:128], U[:, segs[4]:segs[4] + 15], [U_t], [ps_t[B_TR]])
                cp("dve", pst[0:15, 4, :], ps[B_TR][0:15, 0:128], [ps_t[B_TR]], [pst_t])
                dma(pp[l, :, c * 128:(c + 1) * 128], pst[0:15, 0, :], [pst_t], [], is_out=True)
                dma(pls[l, :, :, c * 128:(c + 1) * 128].rearrange("s r n -> r s n"), pst[0:15, 1:5, :], [pst_t], [],
                    is_out=True)
                cp("pool", us[:, :, 0:15], ppre[:, :, c, 0:15], [pre_t], [us_t])
                cp("pool", us[:, :, 15:79], U[:, TP:T].rearrange("p (s n) -> p s n", s=4), [U_t], [us_t])
                src, srct = U, U_t
                ssrc, ssrct = us, us_t
                bufs = [(X, X_t), (Y, Y_t)]
                sbufs = [(ux, ux_t), (uy, uy_t)]
                sh = 1
                for lev in range(nlev):
                    lo = 2 * sh - 1
                    dst, dstt = bufs[lev % 2]
                    tt("dve", dst[:, lo:T], src[:, lo:T], src[:, lo - sh:T - sh], ALU.add, [srct], [dstt])
                    sdst, sdstt = sbufs[lev % 2]
                    tt("pool", sdst[:, :, lo:79], ssrc[:, :, lo:79], ssrc[:, :, lo - sh:79 - sh], ALU.add,
                       [ssrct], [sdstt])
                    src, srct = dst, dstt
                    ssrc, ssrct = sdst, sdstt
                    sh *= 2
                memset("pool", Dd[:, j, 0:16], 0.0, D_t[j])
                stt("dve", Dd[:, j, 16:TP], src[:, 16:TP], 1.0 / w, U[:, 16:TP], ALU.mult, ALU.subtract,
                    [srct, U_t], D_t[j])
                tt("dve", X[:, 0:16] if src is not X else Y[:, 0:16], src[:, 112:128], cst[:, g * 16:g * 16 + 16],
                   ALU.mult, [srct, c_t], [X_t if src is not X else Y_t])
                fixb = X if src is not X else Y
                fixt = X_t if src is not X else Y_t
                tt("dve", Dd[:, j, 112:128], fixb[:, 0:16], U[:, 112:128], ALU.subtract, [fixt, U_t], D_t[j])
                stt("dve", Dd[:, j, TP:T].rearrange("p (s n) -> p s n", s=4), ssrc[:, :, 15:79], 1.0 / w,
                    us[:, :, 15:79], ALU.mult, ALU.subtract, [ssrct, us_t], D_t[j])
            for e in range(2):
                co = 2 * g + e
                wp_, wpt = W.get()
                wg, wgt = W.get()
                for ti, (c0, n) in enumerate(tiles):
                    pa, pb = [(B_PA, B_PB), (B_Z0, B_Z1), (B_AV, B_X)][ti % 3]
                    wi_ = [0, 1, 6][ti % 3]
                    for k in range(2):
                        mm(ps[pa][:, 0:n], wp_[:, k, :], Dd[:, k, c0:c0 + n], k == 0, k == 1,
                           [wpt] + D_t[k], [ps_t[pa]])
                    proj(wg, wgt, c0, n, pb)
                    act(wk[wi_][:, 0:n], ps[pb][:, 0:n], AF.Silu, [ps_t[pb]], [wk_t[wi_]])
                    stt("dve", yT[:, co, c0:c0 + n], ps[pa][:, 0:n], vec[:, V_PS + l * 8 + co:V_PS + l * 8 + co + 1],
                        wk[wi_][:, 0:n], ALU.mult, ALU.mult, [ps_t[pa], wk_t[wi_], c_t], yw(co, c0, n))
        return [U_t, X_t, Y_t, us_t, ux_t, uy_t, pre_t, pst_t]

    def branch_b(l, ph):
        def a(name, shape, dt):
            return ph.enter_context(nc.sbuf_tensor(name + "_%d" % l, shape, dt))
        knT = a("pb_knT", [128, T], BF16)
        Vtok = a("pb_Vtok", [128, NBP, 128], BF16)
        Vs = a("pb_Vs", [64, 4, 128], BF16)
        qn = a("pb_qn", [128, 512], BF16)
        sp = [a("pb_sp%d" % i, [128, 512], BF16) for i in range(4)]
        wT = [a("pb_wT%d" % i, [128, 512], BF16) for i in range(4)]
        ew = [a("pb_ew%d" % i, [128, 512], F32) for i in range(4)]
        sacc = a("pb_sacc", [128, 512], BF16)
        sacc1 = a("pb_sacc1", [128, 512], BF16)
        cs = a("pb_cs", [128, 4, 64], BF16)
        kst = [a("pb_kst%d" % i, [128, 4, 128], F32) for i in range(2)]
        vst = [a("pb_vst%d" % i, [128, 4, 128], F32) for i in range(2)]
        kcT = [a("pb_kcT%d" % i, [128, 512], BF16) for i in range(2)]
        vc = [a("pb_vc%d" % i, [128, 4, 128], BF16) for i in range(2)]
        knT_t, V_t, Vs_t, qn_t, sacc_t, cs_t = [PT() for _ in range(6)]
        sp_t = [PT() for _ in range(4)]
        wT_t = [PT() for _ in range(4)]
        ew_t = [PT() for _ in range(4)]
        sacc1_t = PT()
        ZB = [B_Z0, B_Z1, B_X, B_TR]
        kst_t = [PT(), PT()]
        vst_t = [PT(), PT()]
        kcT_t = [PT(), PT()]
        vc_t = [PT(), PT()]
        alltr = [knT_t, V_t, Vs_t, qn_t, sacc_t, sacc1_t, cs_t] + sp_t + wT_t + ew_t + kst_t + vst_t + kcT_t + vc_t
        stgi = [0]

        def out_rows(dst_p, dst_s, l, hh, srcbuf, srct, c0, n):
            pass

        for hh in range(int(os.environ.get('KH', '8'))):
            KV = int(os.environ.get('KV', '9'))
            for which in range(2):
                wslot, wt_ = W.get()
                dst_p, dst_s = (kp, ks) if which == 0 else (vp, vs)
                for ti, (c0, n) in (enumerate(tiles) if KV >= 1 else []):
                    bank = B_PA + (ti % 2)
                    proj(wslot, wt_, c0, n, bank)
                    xi_ = 0 if ti % 2 == 0 else 6
                    qi_ = 1 if ti % 2 == 0 else 7
                    xf, xft = wk[xi_], wk_t[xi_]
                    cp("act", xf[:, 0:n], ps[bank][:, 0:n], [ps_t[bank]], [xft])
                    if which == 0:
                        act(wk[qi_][:, 0:n], xf[:, 0:n], AF.Square, [xft], [wk_t[qi_]])
                        mm(ps[B_ST][:, 0:n], onesf[:], wk[qi_][:, 0:n], True, True, [wk_t[qi_], c_t], [ps_t[B_ST]])
                        rstd_from(B_ST, n, 1.0 / 128.0, 4)
                        stt("dve", xf[:, 0:n], xf[:, 0:n], vec[:, V_GK + l:V_GK + l + 1], wk[4][:, 0:n],
                            ALU.mult, ALU.mult, [xft, wk_t[4], c_t], [xft])
                        cp("pool", knT[:, c0:c0 + n], xf[:, 0:n], [xft], [knT_t])
                    for b in (blk(c0, n) if (KV >= 2 and int(os.environ.get('KW', which)) == which) else []):
                        o0 = b * 128 - c0
                        si = stgi[0] % 2
                        stgi[0] += 1
                        if (b < NBP and KV == 7) or (b >= NBP and KV == 6):
                            continue
                        tb_ = B_TR if si == 0 else B_X
                        if b < NBP:
                            tr(ps[tb_][:, 0:128], xf[:, o0:o0 + 128], [xft], [ps_t[tb_]])
                            cp("dve", stg[si][:, 0:128], ps[tb_][:, 0:128], [ps_t[tb_]], [stg_t[si]])
                            if which == 1:
                                cp("pool", Vtok[:, b, :], stg[si][:, 0:128], [stg_t[si]], [V_t])
                            if b == 0 and KV == 3:
                                pass
                            elif b == 0:
                                (lambda *a, **k: None if (KV == 5 or os.environ.get("NODMA")) else dma(*a, **k))(dst_p[l, 0:16, hh * 128:(hh + 1) * 128], stg[si][112:128, 0:128], [stg_t[si]], [],
                                    is_out=True)
                            else:
                                r0 = 16 + (b - 1) * 128
                                (lambda *a, **k: None if (KV == 5 or os.environ.get("NODMA")) else dma(*a, **k))(dst_p[l, r0:r0 + 128, hh * 128:(hh + 1) * 128], stg[si][:, 0:128], [stg_t[si]], [],
                                    is_out=True)
                        else:
                            tr(ps[tb_][:, 0:128], xf[:, o0:o0 + 128], [xft], [ps_t[tb_]])
                            cp("dve", stg[si][:, 0:128], ps[tb_][:, 0:128], [ps_t[tb_]], [stg_t[si]])
                            r0 = (b - NBP) * 128
                            dma(dst_s[l, r0:r0 + 128, hh * 128:(hh + 1) * 128], stg[si][:, 0:128], [stg_t[si]], [],
                                is_out=True)
                if which == 1:
                    for s_ in range(4):
                        k0 = TP + s_ * 64
                        for k in range(8):
                            mm(ps[B_TR][0:64, 0:128], xnT[:, k, k0:k0 + 64], wslot[:, k, :], k == 0, k == 7,
                               [wt_] + xnr(k0, 64), [ps_t[B_TR]])
                        cp("act", Vs[:, s_, :], ps[B_TR][0:64, 0:128], [ps_t[B_TR]], [Vs_t])
            wq, wqt = W.get()
            wg, wgt = W.get()
            KB = int(os.environ.get('KB', '9'))

            def qproj(c0, n):
                proj(wq, wqt, c0, n, B_PA)
                cp("act", wk[0][:, 0:n], ps[B_PA][:, 0:n], [ps_t[B_PA]], [wk_t[0]])
                act(wk[1][:, 0:n], wk[0][:, 0:n], AF.Square, [wk_t[0]], [wk_t[1]])
                mm(ps[B_ST][:, 0:n], onesf[:], wk[1][:, 0:n], True, True, [wk_t[1], c_t], [ps_t[B_ST]])
                rstd_from(B_ST, n, 1.0 / 128.0, 4)
                stt("dve", qn[:, 0:n], wk[0][:, 0:n], gqs[:, l:l + 1], wk[4][:, 0:n], ALU.mult, ALU.mult,
                    [wk_t[0], wk_t[4], c_t], [qn_t])

            def gate_out(c0, n, avcols):
                proj(wg, wgt, c0, n, B_PB)
                act(wk[5][:, 0:n], ps[B_PB][:, 0:n], AF.Silu, [ps_t[B_PB]], [wk_t[5]])
                tt("dve", yT[:, hh, c0:c0 + n], ps[B_AV][:, avcols:avcols + n], wk[5][:, 0:n], ALU.mult,
                   [ps_t[B_AV], wk_t[5]], yw(hh, c0, n))

            for b0 in (range(0, NBP, 4) if KB >= 2 else []):
                b1 = min(b0 + 4, NBP)
                nq = (b1 - b0) * 128
                c0 = b0 * 128
                qproj(c0, nq)
                memset("pool", sacc[:, 0:nq], 0.0, [sacc_t])
                memset("pool", sacc1[:, 0:nq], 0.0, [sacc1_t])
                SB = [(sacc, sacc_t), (sacc1, sacc1_t)]
                kbs = list(range(b1 - 1, -1, -1))
                state = {"first_av": True}

                def stage1(i, kb):
                    pi = i % 4
                    zb = ZB[pi]
                    qo = (max(kb, b0) - b0) * 128
                    mm(ps[zb][:, qo:nq], knT[:, kb * 128:(kb + 1) * 128], qn[:, qo:nq], True, True,
                       [knT_t, qn_t], [ps_t[zb]])
                    act(ew[pi][:, qo:nq], ps[zb][:, qo:nq], AF.Exp, [ps_t[zb]], [ew_t[pi]])
                    act(sp[pi][:, qo:nq], ew[pi][:, qo:nq], AF.Ln, [ew_t[pi]], [sp_t[pi]], bias=1.0)
                    if kb >= b0:
                        tt("pool", sp[pi][:, qo:qo + 128], sp[pi][:, qo:qo + 128], mask01[:], ALU.mult,
                           [sp_t[pi], c_t], [sp_t[pi]])

                def stage2(i, kb):
                    pi = i % 4
                    zb = ZB[pi]
                    qo = (max(kb, b0) - b0) * 128
                    first = (i == 0)
                    mm(ps[zb][:, qo:nq], knT[:, kb * 128:(kb + 1) * 128], qn[:, qo:nq], True, False,
                       [knT_t, qn_t, ew_t[pi]], [ps_t[zb]])
                    mm(ps[zb][:, qo:nq], negtri[:], sp[pi][:, qo:nq], False, first, [sp_t[pi], c_t, ps_t[zb]],
                       [ps_t[zb]])
                    if not first:
                        mm(ps[zb][:, qo:nq], negones[:], SB[i % 2][0][:, qo:nq], False, True,
                           [SB[i % 2][1], c_t, ps_t[zb]], [ps_t[zb]])
                    tt("pool", SB[(i + 1) % 2][0][:, qo:nq], SB[i % 2][0][:, qo:nq], sp[pi][:, qo:nq], ALU.add,
                       [SB[i % 2][1], sp_t[pi]], [SB[(i + 1) % 2][1]])
                    act(wT[pi][:, qo:nq], ps[zb][:, qo:nq], AF.Exp, [ps_t[zb]], [wT_t[pi]])
                    if kb >= b0:
                        tt("pool", wT[pi][:, qo:qo + 128], wT[pi][:, qo:qo + 128], mask01[:], ALU.mult,
                           [wT_t[pi], c_t], [wT_t[pi]])
                    mm(ps[B_AV][:, qo:nq], Vtok[:, kb, :], wT[pi][:, qo:nq], first, i == len(kbs) - 1,
                       [V_t, wT_t[pi], ps_t[B_AV]], [ps_t[B_AV]], sgc=True)

                LA = 2
                for j_ in range(min(LA, len(kbs))):
                    stage1(j_, kbs[j_])
                for i, kb in enumerate(kbs):
                    if i + LA < len(kbs):
                        stage1(i + LA, kbs[i + LA])
                    stage2(i, kb)
                gate_out(c0, nq, 0)

            qproj(TP, 256)
            qns = qn
            for s in (range(4) if KB >= 3 else []):
                qc = qns[:, s * 64:(s + 1) * 64]
                kc0 = TP + s * 64
                pieces = list(range(cfg.NPC - 1, -1, -1))

                def load_piece(i):
                    pc = pieces[i]
                    bi = i % 2
                    dma(kst[bi][:], ck[l, s, pc * 512:(pc + 1) * 512, hh, :].rearrange("(j p) d -> p j d", p=128),
                        [], [kst_t[bi]], q="act")
                    dma(vst[bi][:], cv[l, s, pc * 512:(pc + 1) * 512, hh, :].rearrange("(j p) d -> p j d", p=128),
                        [], [vst_t[bi]], q="act")
                load_piece(0)
                zb = B_Z0
                mm(ps[zb][0:64, 0:64], knT[:, kc0:kc0 + 64], qc, True, True, [knT_t, qn_t], [ps_t[zb]])
                act(ew[0][0:64, 0:64], ps[zb][0:64, 0:64], AF.Exp, [ps_t[zb]], [ew_t[0]])
                act(sp[0][0:64, 0:64], ew[0][0:64, 0:64], AF.Ln, [ew_t[0]], [sp_t[0]], bias=1.0)
                tt("pool", sp[0][0:64, 0:64], sp[0][0:64, 0:64], mask01[0:64, 0:64], ALU.mult, [sp_t[0], c_t], [sp_t[0]])
                mm(ps[zb][0:64, 0:64], knT[:, kc0:kc0 + 64], qc, True, False, [knT_t, qn_t, ew_t[0]], [ps_t[zb]])
                mm(ps[zb][0:64, 0:64], negtri[0:64, 0:64], sp[0][0:64, 0:64], False, True, [sp_t[0], c_t, ps_t[zb]],
                   [ps_t[zb]])
                memset("pool", sacc[:, 0:64], 0.0, [sacc_t])
                cp("pool", sacc[0:64, 0:64], sp[0][0:64, 0:64], [sp_t[0]], [sacc_t])
                act(wT[0][0:64, 0:64], ps[zb][0:64, 0:64], AF.Exp, [ps_t[zb]], [wT_t[0]])
                tt("pool", wT[0][0:64, 0:64], wT[0][0:64, 0:64], mask01[0:64, 0:64], ALU.mult, [wT_t[0], c_t], [wT_t[0]])
                mm(ps[B_AV][:, 0:64], Vs[:, s, :], wT[0][0:64, 0:64], True, False, [Vs_t, wT_t[0]], [ps_t[B_AV]])
                for i, pc in enumerate(pieces):
                    bi = i % 2
                    pi = (i + 1) % 2
                    zb = B_Z0 + pi
                    if i + 1 < len(pieces):
                        load_piece(i + 1)
                    tb_ = B_TR if bi == 0 else B_X
                    for j in range(4):
                        tr(ps[tb_][:, j * 128:(j + 1) * 128], kst[bi][:, j, :], [kst_t[bi]], [ps_t[tb_]])
                    cp("dve", kcT[bi][:], ps[tb_][:], [ps_t[tb_]], [kcT_t[bi]])
                    cp("pool", vc[bi][:], vst[bi][:], [vst_t[bi]], [vc_t[bi]])
                    for j in range(4):
                        mm(ps[zb][:, j * 64:(j + 1) * 64], kcT[bi][:, j * 128:(j + 1) * 128], qc, j == 0, j == 3,
                           [kcT_t[bi], qn_t], [ps_t[zb]])
                    act(ew[pi][:, 0:256], ps[zb][:, 0:256], AF.Exp, [ps_t[zb]], [ew_t[pi]])
                    act(sp[pi][:, 0:256], ew[pi][:, 0:256], AF.Ln, [ew_t[pi]], [sp_t[pi]], bias=1.0)
                    sp3 = sp[pi][:, 0:256].rearrange("p (j n) -> p j n", j=4)
                    cp("pool", cs[:, 3, :], sacc[:, 0:64], [sacc_t], [cs_t])
                    for j in (2, 1, 0):
                        tt("pool", cs[:, j, :], cs[:, j + 1, :], sp3[:, j + 1, :], ALU.add, [cs_t, sp_t[pi]], [cs_t])
                    tt("pool", sacc[:, 0:64], cs[:, 0, :], sp3[:, 0, :], ALU.add, [cs_t, sp_t[pi]], [sacc_t])
                    for j in range(4):
                        mm(ps[zb][:, j * 64:(j + 1) * 64], kcT[bi][:, j * 128:(j + 1) * 128], qc, j == 0, False,
                           [kcT_t[bi], qn_t, ew_t[pi]], [ps_t[zb]])
                    mm(ps[zb][:, 0:256], negtri[:], sp[pi][:, 0:256], False, False, [sp_t[pi], c_t, ps_t[zb]],
                       [ps_t[zb]])
                    mm(ps[zb][:, 0:256], negones[:], cs[:].rearrange("p j n -> p (j n)"), False, True,
                       [cs_t, c_t, ps_t[zb]], [ps_t[zb]])
                    act(wT[pi][:, 0:256], ps[zb][:, 0:256], AF.Exp, [ps_t[zb]], [wT_t[pi]])
                    for j in range(4):
                        mm(ps[B_AV][:, 0:64], vc[bi][:, j, :], wT[pi][:, j * 64:(j + 1) * 64], False,
                           (i == len(pieces) - 1) and j == 3, [vc_t[bi], wT_t[pi], ps_t[B_AV]], [ps_t[B_AV]])
                gate_out(kc0, 64, 0)
        return alltr

    def branch_c(l, ph):
        def a(name, shape, dt):
            return ph.enter_context(nc.sbuf_tensor(name + "_%d" % l, shape, dt))
        Gs = [a("pc_G%d" % i, [128, 640], F32) for i in range(2)]
        kcfs = [a("pc_kcf%d" % i, [128, 512], F32) for i in range(2)]
        ktbs = [a("pc_ktb%d" % i, [128, 512], BF16) for i in range(2)]
        qtbs = [a("pc_qtb%d" % i, [128, 512], BF16) for i in range(2)]
        ktoks = [a("pc_ktok%d" % i, [128, 4, 128], BF16) for i in range(2)]
        vtoks = [a("pc_vtok%d" % i, [128, 4, 128], BF16) for i in range(2)]
        facs = [a("pc_fac%d" % i, [128, 3, 8], F32) for i in range(2)]
        wkx = [a("pc_wk%d" % i, [128, 512], F32) for i in range(6)]
        wkc = [[wk[i] for i in range(6)], wkx]
        Sa = [a("pc_S%d" % i, [128, 128], F32) for i in range(2)]
        Sst = a("pc_Sst", [128, 128], F32)
        Stmp = a("pc_Stmp", [128, 128], F32)
        Sbf = [a("pc_Sbf%d" % i, [128, 128], BF16) for i in range(2)]
        attb = [a("pc_attb%d" % i, [128, 64], BF16) for i in range(2)]
        Sst_t, Stmp_t = PT(), PT()
        Gs_t, kcfs_t, ktbs_t, qtbs_t, ktoks_t, vtoks_t, facs_t = [[PT(), PT()] for _ in range(7)]
        wkx_t = [PT() for _ in range(6)]
        wkc_t = [[wk_t[i] for i in range(6)], wkx_t]
        tcnt = [0]
        Sa_t = [PT(), PT()]
        Sbf_t = [PT(), PT()]
        attb_t = [PT(), PT()]
        alltr = ([Sst_t, Stmp_t] + Sa_t + Sbf_t + attb_t + Gs_t + kcfs_t + ktbs_t + qtbs_t + ktoks_t + vtoks_t
                 + facs_t + wkx_t)
        nchunk_total = T // 64
        for hh in range(8):
            wf, wft = W.get()
            wq, wqt = W.get()
            wi, wit = W.get()
            wg, wgt = W.get()
            col = l * 8 + hh
            oml = lbv[:, 1, col:col + 1]
            lbf = lbv[:, 2, col:col + 1]
            noml = lbv[:, 3, col:col + 1]
            memset("dve", Gs[tcnt[0] % 2][:, 0:1], 0.0, [Gs_t[tcnt[0] % 2]])
            memset("dve", Sa[0][:], 0.0, [Sa_t[0]])
            cur = 0
            sidx = 0
            for ti, (c0, n) in enumerate(tiles):
                nch = n // 64
                par = tcnt[0] % 2
                tcnt[0] += 1
                G, G_t = Gs[par], Gs_t[par]
                Gp, Gp_t = Gs[1 - par], Gs_t[1 - par]
                kcf, kcf_t = kcfs[par], kcfs_t[par]
                ktb, ktb_t = ktbs[par], ktbs_t[par]
                qtb, qtb_t = qtbs[par], qtbs_t[par]
                ktok, ktok_t = ktoks[par], ktoks_t[par]
                vtok, vtok_t = vtoks[par], vtoks_t[par]
                fac, fac_t = facs[par], facs_t[par]
                w, w_t = wkc[par], wkc_t[par]
                proj(wf, wft, c0, n, B_PA)
                act(w[0][:, 0:n], ps[B_PA][:, 0:n], AF.Sigmoid, [ps_t[B_PA]], [w_t[0]])
                ts("dve", w[1][:, 0:n], w[0][:, 0:n], oml, lbf, ALU.mult, ALU.add, [w_t[0], c_t], [w_t[1]])
                act(w[1][:, 0:n], w[1][:, 0:n], AF.Ln, [w_t[1]], [w_t[1]])
                ts("dve", kcf[:, 0:n], w[0][:, 0:n], noml, oml, ALU.mult, ALU.add, [w_t[0], c_t], [kcf_t])
                if ti > 0:
                    pn = tiles[ti - 1][1]
                    cp("dve", G[:, 0:1], Gp[:, pn:pn + 1], [Gp_t], [G_t])
                S.op("dve", lambda h, n=n, G=G, w=w: h.tensor_tensor_scan(G[:, 1:n + 1], onesb[:, 0:n],
                                                                 w[1][:, 0:n], G[:, 0:1], ALU.mult, ALU.add),
                     [w_t[1], G_t, c_t], [G_t])
                G3 = G[:, 1:n + 1].rearrange("p (j t) -> p j t", t=64)
                gmid = G[:, 32:32 + n].rearrange("p (j t) -> p j t", t=64)[:, :, 0:1].to_broadcast([128, nch, 64])
                tt("dve", w[2][:, 0:n].rearrange("p (j t) -> p j t", t=64), G3, gmid, ALU.subtract, [G_t], [w_t[2]])
                act(w[3][:, 0:n], w[2][:, 0:n], AF.Exp, [w_t[2]], [w_t[3]])
                act(w[2][:, 0:n], w[2][:, 0:n], AF.Exp, [w_t[2]], [w_t[2]], scale=-1.0)
                gs = G[:, 0:n].rearrange("p (j t) -> p j t", t=64)[:, :, 0]
                gm = G[:, 32:32 + n].rearrange("p (j t) -> p j t", t=64)[:, :, 0]
                ge = G[:, 64:64 + n].rearrange("p (j t) -> p j t", t=64)[:, :, 0]
                tt("dve", fac[:, 0, 0:nch], ge, gs, ALU.subtract, [G_t], [fac_t])
                tt("dve", fac[:, 1, 0:nch], ge, gm, ALU.subtract, [G_t], [fac_t])
                tt("dve", fac[:, 2, 0:nch], gm, gs, ALU.subtract, [G_t], [fac_t])
                act(fac[:, :, 0:nch], fac[:, :, 0:nch], AF.Exp, [fac_t], [fac_t])
                proj(wq, wqt, c0, n, B_PB)
                act(w[0][:, 0:n], ps[B_PB][:, 0:n], AF.Silu, [ps_t[B_PB]], [w_t[0]])
                tt("dve", qtb[:, 0:n], w[0][:, 0:n], w[3][:, 0:n], ALU.mult, [w_t[0], w_t[3]], [qtb_t])
                tt("dve", kcf[:, 0:n], kcf[:, 0:n], w[2][:, 0:n], ALU.mult, [kcf_t, w_t[2]], [kcf_t])
                cp("pool", ktb[:, 0:n], kcf[:, 0:n], [kcf_t], [ktb_t])
                proj(wi, wit, c0, n, B_PA)
                cp("act", w[1][:, 0:n], ps[B_PA][:, 0:n], [ps_t[B_PA]], [w_t[1]])
                nb4 = n // 128
                for j in range(nb4):
                    tr(ps[B_TR][:, j * 128:(j + 1) * 128], kcf[:, j * 128:(j + 1) * 128], [kcf_t], [ps_t[B_TR]])
                cp("dve", ktok[:, 0:nb4, :], ps[B_TR][:, 0:n].rearrange("p (j n) -> p j n", n=128), [ps_t[B_TR]], [ktok_t])
                for j in range(nb4):
                    tr(ps[B_TR][:, j * 128:(j + 1) * 128], w[1][:, j * 128:(j + 1) * 128], [w_t[1]], [ps_t[B_TR]])
                cp("act", vtok[:, 0:nb4, :], ps[B_TR][:, 0:n].rearrange("p (j n) -> p j n", n=128), [ps_t[B_TR]], [vtok_t])
                for j in range(nch):
                    gch = c0 // 64 + j
                    jb, jr = j // 2, (j % 2) * 64
                    is_smp = gch >= TP // 64
                    if is_smp:
                        s = gch - TP // 64
                        dma(Sst[:], shg[l, s, hh], [], [Sst_t], q="act")
                        Sprev, Sprev_t = Sst, Sst_t
                        Snew, Snew_t = Stmp, Stmp_t
                    else:
                        Sprev, Sprev_t = Sa[cur], Sa_t[cur]
                        Snew, Snew_t = Sa[1 - cur], Sa_t[1 - cur]
                    sb_i = sidx % 2
                    sidx += 1
                    ts("pool", Sbf[sb_i][:], Sprev[:], fac[:, 2, j:j + 1], None, ALU.mult, None, [Sprev_t, fac_t],
                       [Sbf_t[sb_i]])
                    pbk = B_X if j % 2 == 0 else B_Z1
                    abk = B_Z0 if j % 2 == 0 else B_PB
                    mm(ps[pbk][:, 0:128], ktok[jr:jr + 64, jb, :], vtok[jr:jr + 64, jb, :], True, True,
                       [ktok_t, vtok_t], [ps_t[pbk]])
                    ts("dve", Snew[:], Sprev[:], fac[:, 0, j:j + 1], None, ALU.mult, None, [Sprev_t, fac_t, Snew_t], [Snew_t])
                    stt("dve", Snew[:], ps[pbk][:, 0:128], fac[:, 1, j:j + 1], Snew[:], ALU.mult, ALU.add,
                        [ps_t[pbk], fac_t, Snew_t], [Snew_t])
                    mm(ps[abk][jr:jr + 64, 0:64], ktb[:, j * 64:(j + 1) * 64], qtb[:, j * 64:(j + 1) * 64], True, True,
                       [ktb_t, qtb_t], [ps_t[abk]])
                    tt("dve", attb[sb_i][jr:jr + 64, :], ps[abk][jr:jr + 64, 0:64], maskle2[jr:jr + 64, jr // 64, :], ALU.mult,
                       [ps_t[abk], c_t], [attb_t[sb_i]])
                    mm(ps[B_AV][:, j * 64:(j + 1) * 64], vtok[jr:jr + 64, jb, :], attb[sb_i][jr:jr + 64, :], j == 0, False,
                       [vtok_t, attb_t[sb_i], ps_t[B_AV]], [ps_t[B_AV]])
                    mm(ps[B_AV][:, j * 64:(j + 1) * 64], Sbf[sb_i][:], qtb[:, j * 64:(j + 1) * 64], False, j == nch - 1,
                       [Sbf_t[sb_i], qtb_t, ps_t[B_AV]], [ps_t[B_AV]])
                    if is_smp:
                        dma(hs[l, s, hh], Snew[:], [Snew_t], [], is_out=True)
                    else:
                        cur = 1 - cur
                        if gch == TP // 64 - 1:
                            dma(hp[l, hh], Snew[:], [Snew_t], [], is_out=True)
                cp("act", w[0][:, 0:n], ps[B_AV][:, 0:n], [ps_t[B_AV]], [w_t[0]])
                act(w[1][:, 0:n], w[0][:, 0:n], AF.Square, [w_t[0]], [w_t[1]])
                mm(ps[B_ST][:, 0:n], onesf[:], w[1][:, 0:n], True, True, [w_t[1], c_t], [ps_t[B_ST]])
                act(w[4][:, 0:n], ps[B_ST][:, 0:n], AF.Ln, [ps_t[B_ST]], [w_t[4]], bias=EPS, scale=1.0 / 128.0)
                act(w[4][:, 0:n], w[4][:, 0:n], AF.Exp, [w_t[4]], [w_t[4]], scale=-0.5)
                stt("dve", w[0][:, 0:n], w[0][:, 0:n], vec[:, V_GH + l:V_GH + l + 1], w[4][:, 0:n],
                    ALU.mult, ALU.mult, [w_t[0], w_t[4], c_t], [w_t[0]])
                proj(wg, wgt, c0, n, B_PB)
                act(w[5][:, 0:n], ps[B_PB][:, 0:n], AF.Silu, [ps_t[B_PB]], [w_t[5]])
                tt("dve", yT[:, hh, c0:c0 + n], w[0][:, 0:n], w[5][:, 0:n], ALU.mult, [w_t[0], w_t[5]],
                   yw(hh, c0, n))
        return alltr

    def run_phase(fn, l):
        ph = contextlib.ExitStack()
        trks = fn(l, ph)
        last_bar[0] = memset("pool", bar[:], 0.0, trks)
        ph.close()

    for l in range(D):
        plan_layer(l)
    import os
    KSTOP = int(os.environ.get("KSTOP", "99"))
    for l in range(D):
        if KSTOP < 99 and l > 0:
            break
        if KSTOP >= 1:
            norm_phase(l)
        if KSTOP >= 2:
            run_phase(branch_a, l)
        if KSTOP >= 3:
            merge_phase(l, 0)
        if KSTOP >= 4:
            run_phase(branch_b, l)
        if KSTOP >= 5:
            merge_phase(l, 1)
        if KSTOP >= 6:
            run_phase(branch_c, l)
        if KSTOP >= 7:
            merge_phase(l, 2)

    for b in range(1, NB):
        si = b % 2
        ti = (b * 128) // 512
        for half in range(2):
            hb = wk[2 + half]
            hbt = wk_t[2 + half]
            dma(hb[:].rearrange("p (j n) -> p j n", j=4), hd[:, half * 4:half * 4 + 4, b * 128:(b + 1) * 128],
                [hd_t[c][ti] for c in range(half * 4, half * 4 + 4)], [hbt])
            bank = B_PA + half
            for j in range(4):
                tr(ps[bank][:, j * 128:(j + 1) * 128], hb[:, j * 128:(j + 1) * 128], [hbt], [ps_t[bank]])
            cp("act" if half else "dve", stg[si][:, half * 512:(half + 1) * 512], ps[bank][:], [ps_t[bank]], [stg_t[si]])
        if b < NBP:
            dma(yp[(b - 1) * 128:b * 128, :], stg[si][:], [stg_t[si]], [], is_out=True)
        else:
            dma(ys[(b - NBP) * 128:(b - NBP + 1) * 128, :], stg[si][:], [stg_t[si]], [], is_out=True)

    S.emit(st)
    st.close()
    return nc


def host_prep(cfg, inputs):
    D = cfg.DEPTH
    f = lambda a: np.ascontiguousarray(np.asarray(a, dtype=np.float32))

    def pc(v):
        return np.asarray(v, np.float32).reshape(D, 8, 128).transpose(2, 0, 1).reshape(128, D * 8)
    vecs = np.zeros((128, NVEC), np.float32)
    vecs[:, 0:8 * D] = pc(inputs["norm_g"])
    vecs[:, 8 * D:16 * D] = pc(inputs["pool_scale"])
    vecs[:, 16 * D:24 * D] = pc(inputs["hgrn_lower_bounds"])
    vecs[:, 24 * D:25 * D] = np.asarray(inputs["q_norm_g"], np.float32).T
    vecs[:, 25 * D:26 * D] = np.asarray(inputs["k_norm_g"], np.float32).T
    vecs[:, 26 * D:27 * D] = np.asarray(inputs["hgrn_norm_g"], np.float32).T
    consts = np.zeros((128, 64), np.float32)
    for g, w in enumerate((2, 4, 8, 16)):
        consts[:, g * 16:(g + 1) * 16] = 1.0 / np.minimum(np.arange(16) + 1.0, float(w))
    shared = dict(meta=f(inputs["meta_tokens"]), vecs=vecs, consts=consts, w_in=f(inputs["w_in"]),
                  w_pool=f(inputs["w_pool"]), w_branch=f(inputs["w_branch"]), w_out=f(inputs["w_out"]))
    maps = []
    for c in range(8):
        m = dict(shared)
        m["xp"] = f(inputs["x_prompt"][c])
        m["xs"] = f(np.asarray(inputs["x_sample"][4 * c:4 * c + 4]).reshape(256, 1024))
        m["ck"] = f(np.asarray(inputs["cache_k"])[:, 4 * c:4 * c + 4])
        m["cv"] = f(np.asarray(inputs["cache_v"])[:, 4 * c:4 * c + 4])
        m["spool"] = f(np.asarray(inputs["state_pool"])[:, 4 * c:4 * c + 4])
        m["shg"] = f(np.asarray(inputs["state_hgrn"])[:, 4 * c:4 * c + 4])
        maps.append(m)
    return maps


def gather(cfg, res):
    D = cfg.DEPTH
    r = res
    cat = lambda k, ax: np.concatenate([x[k] for x in r], axis=ax)
    y_prompt = np.stack([x["yp"] for x in r], 0)
    y_sample = np.concatenate([x["ys"].reshape(4, 64, 1024) for x in r], 0)
    k_prompt = np.stack([x["kp"].reshape(D, cfg.LP, 8, 128) for x in r], 1)
    v_prompt = np.stack([x["vp"].reshape(D, cfg.LP, 8, 128) for x in r], 1)
    pool_prompt = np.stack([x["pp"] for x in r], 1)
    hgrn_prompt = np.stack([x["hp"] for x in r], 1)
    k_sample = np.concatenate([x["ks"].reshape(D, 4, 64, 8, 128) for x in r], 1)
    v_sample = np.concatenate([x["vs"].reshape(D, 4, 64, 8, 128) for x in r], 1)
    pool_sample = np.concatenate([x["pls"] for x in r], 1)
    hgrn_sample = np.concatenate([x["hs"] for x in r], 1)
    outs = (y_prompt, y_sample, k_prompt, v_prompt, pool_prompt, hgrn_prompt, k_sample, v_sample, pool_sample,
            hgrn_sample)
    return tuple(np.ascontiguousarray(o, dtype=np.float32) for o in outs)


def kernel(**inputs):
    seq = int(np.asarray(inputs["x_prompt"]).shape[1])
    past = int(np.asarray(inputs["cache_k"]).shape[2])
    depth = int(np.asarray(inputs["w_in"]).shape[0])
    cfg = Cfg(seq, past, depth)
    nc = build(cfg)
    maps = host_prep(cfg, inputs)
    res = run_bass_kernel_spmd(nc, maps, core_ids=list(range(8)))
    return gather(cfg, res.results)
```
